# Optimizing a Trainium2 kernel written in Bass

```python
import jax, jax.numpy as jnp
from jax import lax
import numpy as np

D_MODEL = 1024
BATCH = 2
SEQ = 8192
DEPTH = 4

CHUNK = 64
N_MIXERS = 2
GLA_HEADS = 4
GLA_DK = D_MODEL // 2
GLA_DV = D_MODEL
GLA_DK_HEAD = GLA_DK // GLA_HEADS
GLA_DV_HEAD = GLA_DV // GLA_HEADS
GLA_GATE_RANK = 16
GLA_GATE_TAU = 16.0
GLA_IN_WIDTH = 2 * GLA_DK + 2 * GLA_DV + GLA_GATE_RANK
ATT_HEADS = 16
ATT_HEAD_DIM = D_MODEL // ATT_HEADS
LEFT_CHUNKS = 8
BAND = (LEFT_CHUNKS + 1) * CHUNK
MAX_REL = 128
N_REL = 2 * MAX_REL + 1
D_FF = 4 * D_MODEL
DEEPNORM_ALPHA = (2.0 * DEPTH) ** 0.25
DEEPNORM_BETA = (8.0 * DEPTH) ** -0.25
LN_EPS = 1e-5
RMS_EPS = 1e-6
NEG_INF = -1e30
N_GLA_LAYERS = (DEPTH + 1) // 2
N_ATT_LAYERS = DEPTH // 2

kernel_name = "hybrid_gla_chunkattn_deepnorm_adaln"


def layer_norm(x, g, b):
    xf = x.astype(jnp.float32)
    mu = jnp.mean(xf, -1, keepdims=True)
    var = jnp.mean(jnp.square(xf - mu), -1, keepdims=True)
    return ((xf - mu) * lax.rsqrt(var + LN_EPS)).astype(x.dtype) * g + b


def gla_mixer(u, w_in, w_gk2, b_gk, g_norm, w_out):
    B_, S_, _ = u.shape
    nC = S_ // CHUNK
    proj = u @ w_in
    q, k, v, g, gk_lr = jnp.split(
        proj, [GLA_DK, 2 * GLA_DK, 2 * GLA_DK + GLA_DV, 2 * GLA_DK + 2 * GLA_DV], axis=-1)
    log_a = jax.nn.log_sigmoid((gk_lr @ w_gk2 + b_gk).astype(jnp.float32)) / GLA_GATE_TAU

    def to_chunks(t, hd):
        return t.reshape(B_, nC, CHUNK, GLA_HEADS, hd).transpose(0, 3, 1, 2, 4)

    qc = to_chunks(q.astype(jnp.float32), GLA_DK_HEAD) * (GLA_DK_HEAD ** -0.5)
    kc = to_chunks(k.astype(jnp.float32), GLA_DK_HEAD)
    vc = to_chunks(v.astype(jnp.float32), GLA_DV_HEAD)
    cum = jnp.cumsum(to_chunks(log_a, GLA_DK_HEAD), axis=3)
    e_pos = jnp.exp(cum)
    e_neg = jnp.exp(-cum)
    q_fwd = qc * e_pos
    a_fwd = jnp.einsum('bhntd,bhnsd->bhnts', q_fwd, kc * e_neg)
    a_bwd = jnp.einsum('bhntd,bhnsd->bhnts', qc * e_neg, kc * e_pos)
    lower = jnp.tril(jnp.ones((CHUNK, CHUNK), dtype=bool))
    o_intra = jnp.einsum('bhnts,bhnsv->bhntv', jnp.where(lower, a_fwd, a_bwd), vc)
    k_to_end = kc * jnp.exp(cum[:, :, :, -1:, :] - cum)
    chunk_decay = jnp.exp(cum[:, :, :, -1, :])

    def step(state, inp):
        qs, ke, vv, dec = inp
        o = jnp.einsum('bhtk,bhkv->bhtv', qs, state)
        state = state * dec[..., None] + jnp.einsum('bhtk,bhtv->bhkv', ke, vv)
        return state, o

    s0 = jnp.zeros((B_, GLA_HEADS, GLA_DK_HEAD, GLA_DV_HEAD), jnp.float32)
    _, o_inter = lax.scan(step, s0, (jnp.moveaxis(q_fwd, 2, 0), jnp.moveaxis(k_to_end, 2, 0),
                                     jnp.moveaxis(vc, 2, 0), jnp.moveaxis(chunk_decay, 2, 0)))
    o = o_intra + jnp.moveaxis(o_inter, 0, 2)
    o = o * lax.rsqrt(jnp.mean(jnp.square(o), -1, keepdims=True) + RMS_EPS)
    o = o * g_norm.astype(jnp.float32)[None, :, None, None, :]
    o = o.transpose(0, 2, 3, 1, 4).reshape(B_, S_, GLA_DV).astype(u.dtype)
    return (o * jax.nn.silu(g)) @ w_out


def chunk_attention(u, w_in, b_in, rel_bias, w_out):
    B_, S_, _ = u.shape
    nC = S_ // CHUNK
    q, k, v = jnp.split(u @ w_in + b_in, 3, axis=-1)

    def heads(t):
        return t.reshape(B_, S_, ATT_HEADS, ATT_HEAD_DIM).transpose(0, 2, 1, 3)

    pad = LEFT_CHUNKS * CHUNK
    qc = (heads(q) * (ATT_HEAD_DIM ** -0.5)).reshape(B_, ATT_HEADS, nC, CHUNK, ATT_HEAD_DIM)
    kp = jnp.pad(heads(k), ((0, 0), (0, 0), (pad, 0), (0, 0)))
    vp = jnp.pad(heads(v), ((0, 0), (0, 0), (pad, 0), (0, 0)))
    key_valid = jnp.arange(S_ + pad) >= pad
    rel = jnp.clip(pad + jnp.arange(CHUNK)[:, None] - jnp.arange(BAND)[None, :], -MAX_REL, MAX_REL) + MAX_REL
    bias = rel_bias.astype(jnp.float32)[:, rel]

    def one_chunk(inp):
        qi, start = inp
        kb = lax.dynamic_slice_in_dim(kp, start, BAND, axis=2)
        vb = lax.dynamic_slice_in_dim(vp, start, BAND, axis=2)
        mb = lax.dynamic_slice_in_dim(key_valid, start, BAND)
        s = jnp.einsum('bhtd,bhjd->bhtj', qi, kb).astype(jnp.float32) + bias
        p = jax.nn.softmax(jnp.where(mb, s, NEG_INF), axis=-1).astype(vb.dtype)
        return jnp.einsum('bhtj,bhjd->bhtd', p, vb)

    o = lax.map(one_chunk, (jnp.moveaxis(qc, 2, 0), jnp.arange(nC) * CHUNK))
    o = o.transpose(1, 0, 3, 2, 4).reshape(B_, S_, D_MODEL)
    return o @ w_out


def squared_relu_mlp(u, w1, w2):
    return jnp.square(jax.nn.relu(u @ w1)) @ w2


def setup_inputs(seed: int = 0) -> dict:
    key = jax.random.key(seed)
    ks = jax.random.split(key, 16)
    nrm = lambda k, shape, s: jax.random.normal(k, shape, jnp.float32) * s
    D = D_MODEL
    return {
        "x": nrm(ks[0], (BATCH, SEQ, D), 1.0),
        "c": nrm(ks[1], (BATCH, D), 1.0),
        "w_ada": nrm(ks[2], (DEPTH, D, 6 * D), 0.1 * D ** -0.5),
        "b_ada": nrm(ks[3], (DEPTH, 6 * D), 0.02),
        "ln_g": 1.0 + nrm(ks[4], (DEPTH, 2, D), 0.02),
        "ln_b": nrm(ks[5], (DEPTH, 2, D), 0.02),
        "gla_w_in": nrm(ks[6], (N_GLA_LAYERS, D, GLA_IN_WIDTH), D ** -0.5),
        "gla_w_gk2": nrm(ks[7], (N_GLA_LAYERS, GLA_GATE_RANK, GLA_DK), GLA_GATE_RANK ** -0.5),
        "gla_b_gk": nrm(ks[8], (N_GLA_LAYERS, GLA_DK), 0.1),
        "gla_g_norm": 1.0 + nrm(ks[9], (N_GLA_LAYERS, GLA_HEADS, GLA_DV_HEAD), 0.02),
        "gla_w_out": nrm(ks[10], (N_GLA_LAYERS, GLA_DV, D), DEEPNORM_BETA * GLA_DV ** -0.5),
        "att_w_in": nrm(ks[11], (N_ATT_LAYERS, D, 3 * D), D ** -0.5),
        "att_b_in": nrm(ks[12], (N_ATT_LAYERS, 3 * D), 0.02),
        "att_rel_bias": nrm(ks[13], (N_ATT_LAYERS, ATT_HEADS, N_REL), 0.2),
        "att_w_out": nrm(ks[14], (N_ATT_LAYERS, D, D), DEEPNORM_BETA * D ** -0.5),
        "ff_w1": nrm(jax.random.fold_in(ks[15], 0), (DEPTH, D, D_FF), D ** -0.5),
        "ff_w2": nrm(jax.random.fold_in(ks[15], 1), (DEPTH, D_FF, D), DEEPNORM_BETA * D_FF ** -0.5),
    }


def reference(x, c, w_ada, b_ada, ln_g, ln_b, gla_w_in, gla_w_gk2, gla_b_gk, gla_g_norm, gla_w_out,
              att_w_in, att_b_in, att_rel_bias, att_w_out, ff_w1, ff_w2):
    c_act = jax.nn.silu(c)
    for i in range(DEPTH):
        mods = jnp.split(c_act @ w_ada[i] + b_ada[i], 6, axis=-1)
        sh1, sc1, g1, sh2, sc2, g2 = [m[:, None, :] for m in mods]
        u = x * (1.0 + sc1) + sh1
        j = i // N_MIXERS
        if i % N_MIXERS == 0:
            y = gla_mixer(u, gla_w_in[j], gla_w_gk2[j], gla_b_gk[j], gla_g_norm[j], gla_w_out[j])
        else:
            y = chunk_attention(u, att_w_in[j], att_b_in[j], att_rel_bias[j], att_w_out[j])
        x = layer_norm(DEEPNORM_ALPHA * x + (1.0 + g1) * y, ln_g[i, 0], ln_b[i, 0])
        u = x * (1.0 + sc2) + sh2
        y = squared_relu_mlp(u, ff_w1[i], ff_w2[i])
        x = layer_norm(DEEPNORM_ALPHA * x + (1.0 + g2) * y, ln_g[i, 1], ln_b[i, 1])
    return x
```

```python
import numpy as np
from contextlib import ExitStack
import concourse.bass as bass
import concourse.mybir as mybir
from concourse.bass_utils import run_bass_kernel_spmd

F32 = mybir.dt.float32
BF16 = mybir.dt.bfloat16
ALU = mybir.AluOpType
AF = mybir.ActivationFunctionType
AX = mybir.AxisListType

D = 1024
NCH = 8
T = 2048
TG = 512
NG = T // TG
HALO = 512
TE = T + HALO
DEPTH = 4
DFF = 4096
ALPHA = (2.0 * DEPTH) ** 0.25
LN_EPS_EFF = 1e-5 / (ALPHA * ALPHA)
RMS_EPS = 1e-6
NEG = -30000.0
DEBUG = False
GLA_W = 3088


class Eng:
    def __init__(self, nc, eng, name):
        self.nc, self.eng, self.name = nc, eng, name
        self.sem = nc.alloc_semaphore("sem_" + name)
        self.cnt = 0
        self.seen = {}
        self.self_sync = name in ("act", "dve", "pool")

    def mark(self, ins):
        ins.then_inc(self.sem, 1)
        self.cnt += 1
        return (self, self.cnt)

    def wait(self, tok):
        if tok is None:
            return
        prod, val = tok
        if prod is self and not self.self_sync:
            return
        if self.seen.get(id(prod), 0) >= val:
            return
        self.eng.wait_ge(prod.sem, val)
        self.seen[id(prod)] = val


class DSem:
    def __init__(self, nc, name):
        self.sem = nc.alloc_semaphore(name)
        self.total = 0


class Buf:
    def __init__(self, name):
        self.name = name
        self.ready = None
        self.readers = {}


class KB:
    def __init__(self):
        self.nc = nc = bass.Bass("TRN2", target_bir_lowering=False)
        self.pe = Eng(nc, nc.tensor, "pe")
        self.act = Eng(nc, nc.scalar, "act")
        self.dve = Eng(nc, nc.vector, "dve")
        self.pool = Eng(nc, nc.gpsimd, "pool")
        self.sp = Eng(nc, nc.sync, "sp")
        self.dsems = {"sp": [DSem(nc, f"dsp{i}") for i in range(12)],
                      "pool": [DSem(nc, f"dpl{i}") for i in range(12)]}
        self.dnext = {"sp": 0, "pool": 0}
        self.ccsem = DSem(nc, "ccsem")
        self.out_toks = []
        self.bufs = {}

    def buf(self, name):
        b = Buf(name)
        return b

    def _pre(self, E, reads, writes):
        for b in reads:
            E.wait(b.ready)
        for b in writes:
            E.wait(b.ready)
            for t in list(b.readers.values()):
                E.wait(t)

    def _post(self, tok, reads, writes):
        for b in writes:
            b.ready = tok
            b.readers = {}
        for b in reads:
            b.readers[id(tok[0])] = tok

    def op(self, E, fn, reads=(), writes=(), mark=True):
        self._pre(E, reads, writes)
        ins = fn()
        if not mark:
            return None
        tok = E.mark(ins)
        self._post(tok, reads, writes)
        return tok

    def dma(self, E, out_ap, in_ap, reads=(), writes=(), is_output=False):
        self._pre(E, reads, writes)
        lst = self.dsems[E.name]
        ds = lst[self.dnext[E.name] % len(lst)]
        self.dnext[E.name] += 1
        if ds.total > 0:
            E.wait((ds, ds.total))
        ins = E.eng.dma_start(out=out_ap, in_=in_ap)
        ins.then_inc(ds.sem, 16)
        ds.total += 16
        tok = (ds, ds.total)
        self._post(tok, reads, writes)
        if is_output:
            self.out_toks.append(tok)
        return tok

    def collective(self, in_ap, out_ap, groups, wait=True):
        E = self.pool
        ins = self.nc.gpsimd.collective_compute("AllGather", ALU.bypass, replica_groups=groups, ins=[in_ap], outs=[out_ap])
        ins.then_inc(self.ccsem.sem, 1)
        self.ccsem.total += 1
        tok = (self.ccsem, self.ccsem.total)
        if wait:
            E.wait(tok)
        return tok

    def barrier(self):
        engs = [self.pe, self.act, self.dve, self.pool, self.sp]
        snap = [(P, P.cnt) for P in engs if P.cnt > 0]
        dsn = [(ds, ds.total) for lst in self.dsems.values() for ds in lst if ds.total > 0]
        if self.ccsem.total > 0:
            dsn.append((self.ccsem, self.ccsem.total))
        for E in engs:
            for tok in snap + dsn:
                E.wait(tok)

    def finish(self):
        for tok in self.out_toks:
            self.sp.wait(tok)
        for lst in self.dsems.values():
            for ds in lst:
                if ds.total > 0:
                    self.sp.wait((ds, ds.total))


_UID = [0]


def sb(nc, es, name, shape, dt):
    _UID[0] += 1
    return es.enter_context(nc.sbuf_tensor(f"s{_UID[0]}_" + name, shape, dt))


class Common:
    def __init__(self, kb, es, io):
        self.kb = kb
        nc = kb.nc
        self.nc = nc
        self.io = io
        self.ps = es.enter_context(nc.psum_tensor("ps", [128, 3072], F32))
        self.psb = es.enter_context(nc.psum_tensor("psb", [128, 2048], BF16))
        self.bank = [kb.buf(f"bank{i}") for i in range(6)]
        self.bbank = [kb.buf(f"bbank{i}") for i in range(2)]
        self.mods = sb(nc, es, "mods", [128, 48], F32)
        self.lnp = sb(nc, es, "lnp", [128, 32], F32)
        self.der = sb(nc, es, "der", [128, 64], F32)
        self.onesD = sb(nc, es, "onesD", [128, 128], BF16)
        self.ident = sb(nc, es, "ident", [128, 128], BF16)
        self.eps_ln = sb(nc, es, "eps_ln", [128, 1], F32)
        self.uid = 0
        self.b_small = kb.buf("small")
        kb.dma(kb.pool, self.onesD[:], io["onesD"], writes=[self.b_small])
        kb.dma(kb.pool, self.ident[:], io["ident"], writes=[self.b_small])
        if "mods_pc" in io:
            self.load_layer(io["mods_pc"], io["lnp"])

    def load_layer(self, mods_ap, lnp_ap, mods_view=None):
        kb, nc = self.kb, self.nc
        if mods_view is None:
            kb.dma(kb.sp, self.mods[:], mods_ap, writes=[self.b_small])
        else:
            kb.dma(kb.sp, mods_view(self.mods), mods_ap, writes=[self.b_small])
        kb.dma(kb.sp, self.lnp[:], lnp_ap, writes=[self.b_small])
        m, l, d = self.mods, self.lnp, self.der
        v = nc.vector
        B = [self.b_small]
        kb.op(kb.dve, lambda: v.tensor_scalar(out=d[:, 0:8], in0=m[:, 8:16], scalar1=1.0, scalar2=None, op0=ALU.add), B, B)
        kb.op(kb.dve, lambda: v.tensor_copy(out=d[:, 8:16], in_=m[:, 0:8]), B, B)
        kb.op(kb.dve, lambda: v.tensor_scalar(out=d[:, 16:24], in0=m[:, 16:24], scalar1=1.0, scalar2=1.0 / ALPHA, op0=ALU.add, op1=ALU.mult), B, B)
        kb.op(kb.dve, lambda: v.tensor_scalar(out=d[:, 48:56], in0=m[:, 32:40], scalar1=1.0, scalar2=None, op0=ALU.add), B, B)
        kb.op(kb.dve, lambda: v.tensor_tensor(out=d[:, 24:32], in0=d[:, 48:56], in1=l[:, 0:8], op=ALU.mult), B, B)
        kb.op(kb.dve, lambda: v.tensor_tensor(out=d[:, 32:40], in0=d[:, 48:56], in1=l[:, 8:16], op=ALU.mult), B, B)
        kb.op(kb.dve, lambda: v.tensor_tensor(out=d[:, 32:40], in0=d[:, 32:40], in1=m[:, 24:32], op=ALU.add), B, B)
        kb.op(kb.dve, lambda: v.tensor_scalar(out=d[:, 40:48], in0=m[:, 40:48], scalar1=1.0, scalar2=1.0 / ALPHA, op0=ALU.add, op1=ALU.mult), B, B)

    def bank_ap(self, i, n=512):
        return self.ps[:, i * 512:i * 512 + n]


LH = 256
NH = T // LH


def ln_pipeline(cm, xres, xbh, gcol, bcol, tmps, out_u=None, ubufs=None, acol=None, bcol2=None, out_dram=None, hooks=None):
    kb, nc = cm.kb, cm.nc
    v = nc.vector
    hooks = hooks or {}

    def sl(j):
        return slice(j * LH, (j + 1) * LH)

    def l1(j):
        t = tmps[j % 2]
        z = xres[:, :, sl(j)]
        kb.op(kb.act, lambda: nc.scalar.activation(out=t["zb"][:], in_=z, func=AF.Identity), [xbh[j]], [t["bzb"]])
        kb.op(kb.act, lambda: nc.scalar.activation(out=t["zsq"][:], in_=z, func=AF.Square), [xbh[j]], [t["bzsq"]])

    def l2(j):
        t = tmps[j % 2]
        bk = cm.bank[4 + j % 2]
        c0 = (4 + j % 2) * 512
        kb._pre(kb.pe, [t["bzb"], t["bzsq"], cm.b_small], [bk])
        for c in range(NCH):
            nc.tensor.matmul(cm.ps[:, c0:c0 + LH], cm.onesD[:], t["zb"][:, c, :], start=(c == 0), stop=(c == NCH - 1))
        for c in range(NCH):
            ins = nc.tensor.matmul(cm.ps[:, c0 + LH:c0 + 2 * LH], cm.onesD[:], t["zsq"][:, c, :], start=(c == 0), stop=(c == NCH - 1))
        tok = kb.pe.mark(ins)
        kb._post(tok, [t["bzb"], t["bzsq"]], [bk])

    def l3(j):
        t = tmps[j % 2]
        bk = cm.bank[4 + j % 2]
        c0 = (4 + j % 2) * 512
        st, bst = t["st"], t["bst"]
        mean, t2, rstd = st[:, 0, :], st[:, 1, :], st[:, 2, :]
        kb.op(kb.dve, lambda: v.tensor_copy(out=mean, in_=cm.ps[:, c0:c0 + LH]), [bk], [bst])
        kb.op(kb.dve, lambda: v.tensor_tensor(out=t2, in0=mean, in1=mean, op=ALU.mult), [bst], [bst])
        kb.op(kb.dve, lambda: v.tensor_tensor(out=t2, in0=cm.ps[:, c0 + LH:c0 + 2 * LH], in1=t2, op=ALU.subtract), [bk, bst], [bst])
        kb.op(kb.act, lambda: nc.scalar.activation(out=t2, in_=t2, func=AF.Ln, bias=cm.eps_ln[:, 0:1], scale=1.0), [bst, cm.b_small], [bst])
        kb.op(kb.act, lambda: nc.scalar.activation(out=rstd, in_=t2, func=AF.Exp, scale=-0.5), [bst], [bst])

    def l4(j):
        t = tmps[j % 2]
        st, bst = t["st"], t["bst"]
        z = xres[:, :, sl(j)]
        mb = st[:, 0, :].unsqueeze(1).broadcast_to([128, NCH, LH])
        rb = st[:, 2, :].unsqueeze(1).broadcast_to([128, NCH, LH])
        kb.op(kb.dve, lambda: v.tensor_tensor(out=z, in0=z, in1=mb, op=ALU.subtract), [xbh[j], bst], [xbh[j]])
        kb.op(kb.dve, lambda: v.tensor_tensor(out=z, in0=z, in1=rb, op=ALU.mult), [xbh[j], bst], [xbh[j]])

    def l5(j):
        if out_u is not None:
            for c in range(NCH):
                kb.op(kb.act, lambda c=c: nc.scalar.activation(
                    out=out_u[:, c, sl(j)], in_=xres[:, c, sl(j)], func=AF.Identity,
                    scale=cm.der[:, acol + c:acol + c + 1], bias=cm.der[:, bcol2 + c:bcol2 + c + 1]), [xbh[j], cm.b_small], [ubufs[j // 2]])
        for c in range(NCH):
            kb.op(kb.dve, lambda c=c: v.tensor_scalar(
                out=xres[:, c, sl(j)], in0=xres[:, c, sl(j)],
                scalar1=cm.lnp[:, gcol + c:gcol + c + 1], scalar2=cm.lnp[:, bcol + c:bcol + c + 1],
                op0=ALU.mult, op1=ALU.add), [xbh[j], cm.b_small], [xbh[j]])
        if out_dram is not None:
            kb.dma(kb.sp, out_dram[:, :, sl(j)], xres[:, :, sl(j)], reads=[xbh[j]], is_output=True)

    stages = [l1, l2, l3, l4, l5]
    for step in range(NH + len(stages) - 1):
        if step in hooks:
            hooks[step]()
        for off, fn in enumerate(stages):
            j = step - off
            if 0 <= j < NH:
                fn(j)


def emit_outproj_ln_ffn(cm, es, mixT, bmix, x_src, x_off, io=None, row_scale=None):
    kb, nc = cm.kb, cm.nc
    if io is None:
        io = cm.io
    v = nc.vector
    xres = sb(nc, es, "xres", [128, NCH, T], F32)
    xbh = [kb.buf(f"xres{j}") for j in range(NH)]
    xb2 = lambda g: [xbh[2 * g], xbh[2 * g + 1]]
    u2 = sb(nc, es, "u2", [128, NCH, T], BF16)
    u2b = [kb.buf(f"u2_{g}") for g in range(NG)]
    wslot = [sb(nc, es, f"wslot{i}", [128, 8192], BF16) for i in range(2)]
    wsb = [kb.buf(f"wslot{i}") for i in range(2)]
    w1s = [w[:, 0:4096].rearrange("p (c f) -> p c f", c=8) for w in wslot]
    w2s = [w[:, 4096:8192].rearrange("p (c f) -> p c f", c=4) for w in wslot]
    wos = wslot[0][:, :].rearrange("p (c f) -> p c f", c=8)
    hT = [sb(nc, es, f"hT{i}", [128, 4, TG], BF16) for i in range(2)]
    hTb = [kb.buf(f"hT{i}") for i in range(2)]
    rl = [sb(nc, es, f"rl{i}", [128, TG], BF16) for i in range(2)]
    rlb = [kb.buf(f"rl{i}") for i in range(2)]
    tmps = []
    for i_ in range(2):
        tmps.append({"zb": sb(nc, es, f"zb{i_}", [128, NCH, LH], BF16), "zsq": sb(nc, es, f"zsq{i_}", [128, NCH, LH], BF16),
                     "st": sb(nc, es, f"lnst{i_}", [128, 3, LH], F32),
                     "bzb": kb.buf(f"zb{i_}"), "bzsq": kb.buf(f"zsq{i_}"), "bst": kb.buf(f"lnst{i_}")})
    kb.op(kb.dve, lambda: v.memset(cm.eps_ln[:], LN_EPS_EFF), [], [cm.b_small])

    xsrc = x_src.rearrange("(c p) t -> p c t", p=128)
    for g in range(NG):
        kb.dma(kb.sp, xres[:, :, g * TG:(g + 1) * TG], xsrc[:, :, x_off + g * TG:x_off + (g + 1) * TG], writes=xb2(g))
    kb.dma(kb.pool, wos, io["w_out"].rearrange("(c p) f -> p c f", p=128), writes=[wsb[0]])
    if row_scale is not None:
        rs = sb(nc, es, "rowscale", [128, NCH], F32)
        kb.dma(kb.sp, rs[:], row_scale, writes=[cm.b_small])
        for c in range(NCH):
            kb.op(kb.dve, lambda c=c: v.tensor_scalar(out=wos[:, c, :], in0=wos[:, c, :], scalar1=rs[:, c:c + 1], scalar2=None, op0=ALU.mult),
                  [wsb[0], cm.b_small], [wsb[0]])
    w1v = io["ff_w1"].rearrange("(c p) f -> p c f", p=128)
    w2v = io["ff_w2"].rearrange("(e c p) d -> e p c d", p=128, c=4)
    NE = 8

    def load_ffn_e(e, s):
        kb.dma(kb.pool, w1s[s], w1v[:, :, e * 512:(e + 1) * 512], writes=[wsb[s]])
        kb.dma(kb.pool, w2s[s], w2v[e], writes=[wsb[s]])

    load_ffn_e(0, 1)
    nb = 0

    def outproj(g):
        nonlocal nb
        for dc in range(NCH):
            bk = nb % 4
            nb += 1
            kb._pre(kb.pe, [wsb[0], bmix], [cm.bank[bk]])
            for c in range(NCH):
                ins = nc.tensor.matmul(cm.bank_ap(bk), wos[:, c, dc * 128:(dc + 1) * 128],
                                       mixT[:, c, g * TG:(g + 1) * TG], start=(c == 0), stop=(c == NCH - 1))
            tok = kb.pe.mark(ins)
            kb._post(tok, [wsb[0], bmix], [cm.bank[bk]])
            kb.op(kb.dve, lambda dc=dc, bk=bk, g=g: v.scalar_tensor_tensor(
                out=xres[:, dc, g * TG:(g + 1) * TG], in0=cm.bank_ap(bk), scalar=cm.der[:, 16 + dc:17 + dc],
                in1=xres[:, dc, g * TG:(g + 1) * TG], op0=ALU.mult, op1=ALU.add), [cm.bank[bk], cm.b_small] + xb2(g), xb2(g))

    steps = [(e, g) for e in range(NE) for g in range(NG)]

    def emit_H(si):
        nonlocal nb
        e, g = steps[si]
        s = (e + 1) % 2
        hs = si % 2
        for fc in range(4):
            bk = nb % 4
            nb += 1
            kb._pre(kb.pe, [wsb[s], u2b[g]], [cm.bank[bk]])
            for c in range(NCH):
                ins = nc.tensor.matmul(cm.bank_ap(bk), w1s[s][:, c, fc * 128:(fc + 1) * 128],
                                       u2[:, c, g * TG:(g + 1) * TG], start=(c == 0), stop=(c == NCH - 1))
            tok = kb.pe.mark(ins)
            kb._post(tok, [wsb[s], u2b[g]], [cm.bank[bk]])
            r = fc % 2
            kb.op(kb.act, lambda bk=bk, r=r: nc.scalar.activation(out=rl[r][:], in_=cm.bank_ap(bk), func=AF.Relu),
                  [cm.bank[bk]], [rlb[r]])
            kb.op(kb.dve, lambda fc=fc, r=r, hs=hs: v.tensor_tensor(out=hT[hs][:, fc, :], in0=rl[r][:], in1=rl[r][:], op=ALU.mult),
                  [rlb[r]], [hTb[hs]])

    def emit_Y(si):
        nonlocal nb
        e, g = steps[si]
        s = (e + 1) % 2
        hs = si % 2
        for dc in range(NCH):
            bk = nb % 4
            nb += 1
            kb._pre(kb.pe, [wsb[s], hTb[hs]], [cm.bank[bk]])
            for fc in range(4):
                ins = nc.tensor.matmul(cm.bank_ap(bk), w2s[s][:, fc, dc * 128:(dc + 1) * 128],
                                       hT[hs][:, fc, :], start=(fc == 0), stop=(fc == 3))
            tok = kb.pe.mark(ins)
            kb._post(tok, [wsb[s], hTb[hs]], [cm.bank[bk]])
            kb.op(kb.dve, lambda dc=dc, bk=bk, g=g: v.scalar_tensor_tensor(
                out=xres[:, dc, g * TG:(g + 1) * TG], in0=cm.bank_ap(bk), scalar=cm.der[:, 40 + dc:41 + dc],
                in1=xres[:, dc, g * TG:(g + 1) * TG], op0=ALU.mult, op1=ALU.add), [cm.bank[bk], cm.b_small] + xb2(g), xb2(g))

    xo = io["xT_out"].rearrange("(c p) t -> p c t", p=128)

    def ffn_seq():
        emit_H(0)
        load_ffn_e(1, 0)
        yield
        for si in range(len(steps)):
            e, g = steps[si]
            if si + 1 < len(steps):
                emit_H(si + 1)
            emit_Y(si)
            if g == NG - 1 and e + 2 < NE:
                load_ffn_e(e + 2, (e + 1) % 2)
            yield

    ffn = ffn_seq()
    outproj(0)
    hooks = {0: lambda: outproj(1), 2: lambda: outproj(2), 4: lambda: outproj(3)}
    for st_ in (6, 8, 10):
        hooks[st_] = lambda: next(ffn)
    ln_pipeline(cm, xres, xbh, 0, 8, tmps, out_u=u2, ubufs=u2b, acol=24, bcol2=32, hooks=hooks)
    n_done = 3
    while n_done < 1 + len(steps) - 3:
        next(ffn)
        n_done += 1
    hooks2 = {0: lambda: next(ffn), 2: lambda: next(ffn), 4: lambda: next(ffn)}
    ln_pipeline(cm, xres, xbh, 16, 24, tmps, out_dram=xo, hooks=hooks2)
    for _ in ffn:
        pass


def emit_att(kb, cm, io):
    nc = kb.nc
    with ExitStack() as es0:
        v = nc.vector
        big = sb(nc, es0, "big", [128, NCH, TE], BF16)
        mixT = big[:, :, 0:T]
        bmix = kb.buf("mixT")
        with ExitStack() as es1:
            bqk = sb(nc, es1, "bqk", [128, 16], F32)
            bvb = sb(nc, es1, "bvb", [128, D], F32)
            hb = sb(nc, es1, "hb", [128, HALO], F32)
            kb.dma(kb.sp, bqk[:], io["b_qk"], writes=[cm.b_small])
            kb.dma(kb.sp, bvb[:], io["bv_bc"], writes=[cm.b_small])
            kb.dma(kb.sp, hb[:], io["hb"], writes=[cm.b_small])
            kb.op(kb.dve, lambda: v.tensor_scalar(out=bqk[:, 0:8], in0=bqk[:, 0:8], scalar1=0.125, scalar2=None, op0=ALU.mult),
                  [cm.b_small], [cm.b_small])
            QT = sb(nc, es1, "QT", [128, NCH, T], BF16)
            KT = sb(nc, es1, "KT", [128, NCH, TE], BF16)
            V = sb(nc, es1, "V", [128, TE // 128, D], BF16)
            bQ, bK, bV = kb.buf("QT"), kb.buf("KT"), kb.buf("V")
            with ExitStack() as es2:
                uT = big
                ub = [kb.buf(f"uT{g}") for g in range(TE // TG)]
                xs = [sb(nc, es2, f"xs{i}", [128, NCH, 256], F32) for i in range(2)]
                xsb = [kb.buf(f"xs{i}") for i in range(2)]
                wsl = [sb(nc, es2, f"wsl{i}", [128, NCH, 512], BF16) for i in range(2)]
                wslb = [kb.buf(f"wsl{i}") for i in range(2)]
                xsrc = io["xT"].rearrange("(c p) t -> p c t", p=128)
                hsrc = [io[f"hgat{hh}"].rearrange("(j c p) t -> j p c t", p=128, c=NCH) for hh in range(2)]
                w_in_v = io["w_in"].rearrange("(c p) f -> p c f", p=128)
                xh = sb(nc, es2, "xh", [128, NCH, 256], F32)
                xhb = kb.buf("xh")
                sel = sb(nc, es2, "sel", [128, 4], F32)
                kb.dma(kb.sp, sel[:], io["sel"], writes=[cm.b_small])
                for wb in range(2):
                    kb.dma(kb.pool, wsl[wb][:], w_in_v[:, :, wb * 512:(wb + 1) * 512], writes=[wslb[wb]])
                bhalo = [kb.buf("hgat0"), kb.buf("hgat1")]
                for hh in range(2):
                    bhalo[hh].ready = io["halo_toks"][hh]
                for cnt_, hg in enumerate(list(range(2, TE // 256)) + [0, 1]):
                    s = cnt_ % 2
                    g = hg // 2
                    tsl = slice(hg * 256, (hg + 1) * 256)
                    if hg < 2:
                        for j in range(4):
                            kb.dma(kb.sp, xh[:], hsrc[hg][j], reads=[bhalo[hg]], writes=[xhb])
                            if j == 0:
                                kb.op(kb.dve, lambda s=s: v.tensor_scalar(out=xs[s][:], in0=xh[:], scalar1=sel[:, 0:1], scalar2=None, op0=ALU.mult),
                                      [xhb, cm.b_small], [xsb[s]])
                            else:
                                kb.op(kb.dve, lambda s=s, j=j: v.scalar_tensor_tensor(out=xs[s][:], in0=xh[:], scalar=sel[:, j:j + 1], in1=xs[s][:],
                                                                                 op0=ALU.mult, op1=ALU.add), [xhb, xsb[s], cm.b_small], [xsb[s]])
                    else:
                        kb.dma(kb.sp, xs[s][:], xsrc[:, :, (hg - 2) * 256:(hg - 1) * 256], writes=[xsb[s]])
                    for c in range(NCH):
                        if c % 2 == 0:
                            kb.op(kb.act, lambda c=c, s=s, tsl=tsl: nc.scalar.activation(
                                out=uT[:, c, tsl], in_=xs[s][:, c, :], func=AF.Identity,
                                scale=cm.der[:, c:c + 1], bias=cm.der[:, 8 + c:9 + c]), [xsb[s], cm.b_small], [ub[g]])
                        else:
                            kb.op(kb.dve, lambda c=c, s=s, tsl=tsl: v.tensor_scalar(
                                out=uT[:, c, tsl], in0=xs[s][:, c, :],
                                scalar1=cm.der[:, c:c + 1], scalar2=cm.der[:, 8 + c:9 + c], op0=ALU.mult, op1=ALU.add),
                                [xsb[s], cm.b_small], [ub[g]])
                nb = 0
                for wb in range(6):
                    s = wb % 2
                    if wb < 4:
                        isq = wb < 2
                        groups = [1, 2, 3, 4] if isq else [1, 2, 3, 4, 0]
                        for cc in range(4):
                            oc = (wb % 2) * 4 + cc
                            for g in groups:
                                bk = nb % 4
                                nb += 1
                                kb._pre(kb.pe, [wslb[s], ub[g]], [cm.bank[bk]])
                                for c in range(NCH):
                                    ins = nc.tensor.matmul(cm.bank_ap(bk), wsl[s][:, c, cc * 128:(cc + 1) * 128],
                                                           uT[:, c, g * TG:(g + 1) * TG], start=(c == 0), stop=(c == NCH - 1))
                                tok = kb.pe.mark(ins)
                                kb._post(tok, [wslb[s], ub[g]], [cm.bank[bk]])
                                if isq:
                                    kb.op(kb.act, lambda bk=bk, oc=oc, g=g: nc.scalar.activation(
                                        out=QT[:, oc, (g - 1) * TG:g * TG], in_=cm.bank_ap(bk), func=AF.Identity,
                                        scale=0.125, bias=bqk[:, oc:oc + 1]), [cm.bank[bk], cm.b_small], [bQ])
                                else:
                                    kb.op(kb.act, lambda bk=bk, oc=oc, g=g: nc.scalar.activation(
                                        out=KT[:, oc, g * TG:(g + 1) * TG], in_=cm.bank_ap(bk), func=AF.Identity,
                                        scale=1.0, bias=bqk[:, 8 + oc:9 + oc]), [cm.bank[bk], cm.b_small], [bK])
                    else:
                        vb = wb - 4
                        for tt in list(range(4, TE // 128)) + [0, 1, 2, 3]:
                            g = tt // 4
                            bk = nb % 4
                            nb += 1
                            kb._pre(kb.pe, [wslb[s], ub[g]], [cm.bank[bk]])
                            for c in range(NCH):
                                ins = nc.tensor.matmul(cm.bank_ap(bk), uT[:, c, tt * 128:(tt + 1) * 128],
                                                       wsl[s][:, c, :], start=(c == 0), stop=(c == NCH - 1))
                            tok = kb.pe.mark(ins)
                            kb._post(tok, [wslb[s], ub[g]], [cm.bank[bk]])
                            kb.op(kb.dve, lambda bk=bk, tt=tt, vb=vb: v.tensor_tensor(
                                out=V[:, tt, vb * 512:(vb + 1) * 512], in0=cm.bank_ap(bk), in1=bvb[:, vb * 512:(vb + 1) * 512],
                                op=ALU.add), [cm.bank[bk], cm.b_small], [bV])
                    if wb + 2 < 6:
                        kb.dma(kb.pool, wsl[s][:], w_in_v[:, :, (wb + 2) * 512:(wb + 3) * 512], writes=[wslb[s]])
            kb.barrier()
            with ExitStack() as es3:
                bt = [sb(nc, es3, f"bt{i}", [128, 640], F32) for i in range(2)]
                btb = [kb.buf(f"bt{i}") for i in range(2)]
                ssb = [sb(nc, es3, f"ssb{i}", [128, 640], F32) for i in range(2)]
                ssbb = [kb.buf(f"ssb{i}") for i in range(2)]
                pbf = [sb(nc, es3, f"pbf{i}", [128, 640], BF16) for i in range(2)]
                pbfb = [kb.buf(f"pbf{i}") for i in range(2)]
                pT = [sb(nc, es3, f"pT{i}", [128, 640], BF16) for i in range(2)]
                pTb = [kb.buf(f"pT{i}") for i in range(2)]
                stat = [sb(nc, es3, f"stat{i}", [128, 4], F32) for i in range(2)]
                statb = [kb.buf(f"stat{i}") for i in range(2)]
                its = [(h, i) for h in range(16) for i in range(T // 128)]
                NIT = len(its)
                bt4 = bt + [sb(nc, es3, f"bt{q_}", [128, 640], F32) for q_ in (2, 3)]
                bt4b = btb + [kb.buf("bt2"), kb.buf("bt3")]

                def bslot(h, i):
                    return (h * 5 + min(i, 4)) % 4
                pbf3 = pbf + [sb(nc, es3, "pbf2", [128, 640], BF16)]
                pbf3b = pbfb + [kb.buf("pbf2")]
                stat4 = stat + [sb(nc, es3, f"stat{q_}", [128, 4], F32) for q_ in (2, 3)]
                stat4b = statb + [kb.buf("stat2"), kb.buf("stat3")]

                def st_qk(n):
                    h, i = its[n]
                    k, c, r0, bs = n % 2, h // 2, (h % 2) * 64, h % 2
                    if i <= 4:
                        sl_ = bslot(h, i)
                        src_ = io["bias_halo"][i, h] if i < 4 else io["bias_tiles"][h]
                        kb.dma(kb.sp, bt4[sl_][:], src_, writes=[bt4b[sl_]])
                    bkA, bkB = cm.bank[2 * k], cm.bank[2 * k + 1]
                    kb._pre(kb.pe, [bQ, bK], [bkA, bkB])
                    nc.tensor.matmul(cm.ps[:, 2 * k * 512:2 * k * 512 + 512], QT[r0:r0 + 64, c, i * 128:(i + 1) * 128],
                                     KT[r0:r0 + 64, c, i * 128:i * 128 + 512], start=True, stop=True)
                    ins = nc.tensor.matmul(cm.ps[:, 2 * k * 512 + 512:2 * k * 512 + 640], QT[r0:r0 + 64, c, i * 128:(i + 1) * 128],
                                           KT[r0:r0 + 64, c, i * 128 + 512:i * 128 + 640], start=True, stop=True)
                    tok = kb.pe.mark(ins)
                    kb._post(tok, [bQ, bK], [bkA, bkB])

                def st_front(n):
                    h, i = its[n]
                    k, bs, q4 = n % 2, h % 2, n % 4
                    bkA, bkB = cm.bank[2 * k], cm.bank[2 * k + 1]
                    sl_ = bslot(h, i)
                    kb.op(kb.dve, lambda: v.tensor_tensor(out=ssb[k][:], in0=cm.ps[:, 2 * k * 512:2 * k * 512 + 640],
                                                          in1=bt4[sl_][:], op=ALU.add), [bkA, bkB, bt4b[sl_]], [ssbb[k]])
                    kb.op(kb.dve, lambda: v.reduce_max(out=stat4[q4][:, 0:1], in_=ssb[k][:], axis=AX.X), [ssbb[k]], [stat4b[q4]])
                    kb.op(kb.pool, lambda: nc.gpsimd.tensor_scalar(out=stat4[q4][:, 1:2], in0=stat4[q4][:, 0:1], scalar1=-1.0, scalar2=None, op0=ALU.mult),
                          [stat4b[q4]], [stat4b[q4]])
                    kb.op(kb.pool, lambda: nc.gpsimd.memset(stat4[q4][:, 2:3], 0.0), [], [stat4b[q4]])

                def st_exp(n):
                    k, q4, p3 = n % 2, n % 4, n % 3
                    kb.op(kb.act, lambda: nc.scalar.activation(out=pbf3[p3][:], in_=ssb[k][:], func=AF.Exp, bias=stat4[q4][:, 1:2],
                                                               scale=1.0, accum_out=stat4[q4][:, 2:3]),
                          [ssbb[k], stat4b[q4]], [pbf3b[p3], stat4b[q4]])

                def st_norm(n):
                    q4, p3 = n % 4, n % 3
                    kb.op(kb.dve, lambda: v.reciprocal(out=stat4[q4][:, 3:4], in_=stat4[q4][:, 2:3]), [stat4b[q4]], [stat4b[q4]])
                    kb.op(kb.pool, lambda: nc.gpsimd.tensor_tensor(out=pbf3[p3][:], in0=pbf3[p3][:], in1=stat4[q4][:, 3:4].broadcast_to([128, 640]),
                                                                   op=ALU.mult), [pbf3b[p3], stat4b[q4]], [pbf3b[p3]])

                def st_tr(n):
                    k, p3 = n % 2, n % 3
                    kb._pre(kb.pe, [pbf3b[p3], cm.b_small], [cm.bbank[k]])
                    for jb in range(5):
                        ins = nc.tensor.transpose(cm.psb[:, k * 1024 + jb * 128:k * 1024 + (jb + 1) * 128],
                                                  pbf3[p3][:, jb * 128:(jb + 1) * 128], cm.ident[:])
                    tok = kb.pe.mark(ins)
                    kb._post(tok, [pbf3b[p3]], [cm.bbank[k]])
                    kb.op(kb.act, lambda: nc.scalar.copy(out=pT[k][:], in_=cm.psb[:, k * 1024:k * 1024 + 640]), [cm.bbank[k]], [pTb[k]])

                def st_pv(n):
                    h, i = its[n]
                    k, c, r0 = n % 2, h // 2, (h % 2) * 64
                    ob = cm.bank[4 + k]
                    kb._pre(kb.pe, [pTb[k], bV], [ob])
                    for jb in range(5):
                        ins = nc.tensor.matmul(cm.ps[:, (4 + k) * 512:(4 + k) * 512 + 128], V[:, i + jb, c * 128:(c + 1) * 128],
                                               pT[k][:, jb * 128:(jb + 1) * 128], start=(jb == 0), stop=(jb == 4))
                    tok = kb.pe.mark(ins)
                    kb._post(tok, [pTb[k], bV], [ob])
                    kb.op(kb.act, lambda: nc.scalar.copy(out=mixT[r0:r0 + 64, c, i * 128:(i + 1) * 128],
                                                         in_=cm.ps[r0:r0 + 64, (4 + k) * 512:(4 + k) * 512 + 128]), [ob], [bmix])

                stages = [(st_qk, 0), (st_front, 1), (st_exp, 2), (st_norm, 3), (st_tr, 4), (st_pv, 5)]
                for step in range(NIT + 5):
                    for fn, off in stages:
                        n = step - off
                        if 0 <= n < NIT:
                            fn(n)
                if DEBUG:
                    kb.dma(kb.sp, io["d_QT"], QT[:], reads=[bQ], is_output=True)
                    kb.dma(kb.sp, io["d_KT"], KT[:], reads=[bK], is_output=True)
                    kb.dma(kb.sp, io["d_V"], V[:], reads=[bV], is_output=True)
                    kb.dma(kb.sp, io["d_mix"], mixT, reads=[bmix], is_output=True)
        kb.barrier()
        with ExitStack() as es4:
            emit_outproj_ln_ffn(cm, es4, mixT, bmix, io["xT"], 0, io)
    kb.barrier()


def emit_gla_tail(kb, cm, io):
    nc = kb.nc
    with ExitStack() as es4:
        emit_outproj_ln_ffn(cm, es4, io["mixT_sb"], io["bmix"], io["xT"], 0, io, row_scale=io["gsm"][:, 4:12])
    kb.barrier()


def emit_gla(kb, cm, io, state_only, mixer_only=False):
    nc = kb.nc
    QS = 128.0 ** -0.5

    with ExitStack() as es0:
        v = nc.vector
        if "mixT_sb" in io:
            big = io["mixT_sb"]
        else:
            big = sb(nc, es0, "big", [128, NCH, T], BF16)
        mixT = big
        bmix = kb.buf("mixT")
        io["bmix"] = bmix
        with ExitStack() as es1:
            B = cm.b_small
            gsm = sb(nc, es1, "gsm", [128, 16], F32)
            cmask = sb(nc, es1, "cmask", [128, 512], F32)
            kb.dma(kb.sp, gsm[:], io["gsm"], writes=[B])
            kb.dma(kb.sp, cmask[:], io["cmask"], writes=[B])
            if "w_in_sb" in io:
                w_in, wgk2, bw = io["w_in_sb"], io["wgk2_sb"], io["bw"]
            else:
                w_in = sb(nc, es1, "w_in", [128, NCH, GLA_W], BF16)
                bw = kb.buf("w_in")
                wv_ = io["w_in"].rearrange("(c p) f -> p c f", p=128)
                for c in range(NCH):
                    kb.dma(kb.pool, w_in[:, c, :], wv_[:, c, :], writes=[bw])
                wgk2 = sb(nc, es1, "wgk2", [16, 512], BF16)
                kb.dma(kb.pool, wgk2[:], io["w_gk2"], writes=[bw])
            S = sb(nc, es1, "S", [128, 4, 256], F32)
            bS = [kb.buf(f"S{h}") for h in range(4)]
            Sa = sb(nc, es1, "Sa", [128, 4, 256], F32)
            bSa = [kb.buf(f"Sa{h}") for h in range(4)]
            nsum = sb(nc, es1, "nsum", [128, 8], F32)
            bns = kb.buf("nsum")
            kb.op(kb.dve, lambda: v.memset(nsum[:], 0.0), [], [bns])
            eps_r = sb(nc, es1, "eps_r", [128, 1], F32)
            kb.op(kb.dve, lambda: v.memset(eps_r[:], RMS_EPS), [], [B])
            if state_only:
                for h in range(4):
                    kb.op(kb.dve, lambda h=h: v.memset(S[:, h, :], 0.0), [], [bS[h]])
            else:
                M1 = sb(nc, es1, "M1", [128, 128], F32)
                M2 = sb(nc, es1, "M2", [128, 128], F32)
                ones256 = sb(nc, es1, "ones256", [128, 128], BF16)
                kb.dma(kb.sp, M1[:], io["M1"], writes=[B])
                kb.dma(kb.sp, M2[:], io["M2"], writes=[B])
                kb.dma(kb.pool, ones256[:], io["ones256"], writes=[B])
                pS = sb(nc, es1, "pS", [128, 4, 256], F32)
                pD = sb(nc, es1, "pD", [128, 8], F32)
                gm = sb(nc, es1, "gm", [128, 8], F32)
                bp = kb.buf("pS")
                kb.dma(kb.sp, gm[:, 0:4], io["gm"], writes=[bp])
                kb.dma(kb.sp, gm[:, 4:8], io["gm1"], writes=[bp])
                for h in range(4):
                    kb.op(kb.dve, lambda h=h: v.memset(S[:, h, :], 0.0), [], [bS[h]])
                sdg = io["sd_g"]
                for j in range(3):
                    kb.dma(kb.sp, pS[:].rearrange("p h v -> p (h v)"), sdg[j * 128:(j + 1) * 128, 0:1024], writes=[bp])
                    kb.dma(kb.sp, pD[:, 0:4], sdg[j * 128:(j + 1) * 128, 1024:1028], writes=[bp])
                    kb.op(kb.dve, lambda j=j: v.tensor_scalar(out=pD[:, 4:8], in0=pD[:, 0:4], scalar1=gm[:, j:j + 1], scalar2=gm[:, 4 + j:5 + j],
                                                             op0=ALU.mult, op1=ALU.add), [bp], [bp])
                    kb.op(kb.dve, lambda j=j: v.tensor_scalar(out=pS[:].rearrange("p h v -> p (h v)"), in0=pS[:].rearrange("p h v -> p (h v)"),
                                                             scalar1=gm[:, j:j + 1], scalar2=None, op0=ALU.mult), [bp], [bp])
                    for h in range(4):
                        kb.op(kb.dve, lambda h=h: v.scalar_tensor_tensor(out=S[:, h, :], in0=S[:, h, :], scalar=pD[:, 4 + h:5 + h],
                                                                       in1=pS[:, h, :], op0=ALU.mult, op1=ALU.add), [bp, bS[h]], [bS[h]])
                Sbf = sb(nc, es1, "Sbf", [128, 4, 4, 256], BF16)
                bSbf = [[[kb.buf(f"Sbf{h}_{p_}_{ab}") for ab in range(2)] for p_ in range(2)] for h in range(4)]
                for h in range(4):
                    kb.op(kb.act, lambda h=h: nc.scalar.copy(out=Sbf[:, h, 3, :], in_=S[:, h, :]), [bS[h]], [bSbf[h][1][1]])
            uTg = [sb(nc, es1, f"uTg{i}", [128, NCH, TG], BF16) for i in range(2)]
            bu = [kb.buf(f"uTg{i}") for i in range(2)]
            xs = [sb(nc, es1, f"xs{i}", [128, NCH, 256], F32) for i in range(2)]
            xsb = [kb.buf(f"xs{i}") for i in range(2)]
            gkT = sb(nc, es1, "gkT", [16, 512], BF16)
            bgk = kb.buf("gkT")
            NT = 2
            e1 = [sb(nc, es1, f"e1_{i}", [128, 512], F32) for i in range(NT)]
            ncum = [sb(nc, es1, f"ncum{i}", [128, 512], F32) for i in range(NT)]
            epos = [sb(nc, es1, f"epos{i}", [128, 512], F32) for i in range(NT)]
            eneg = [sb(nc, es1, f"eneg{i}", [128, 512], F32) for i in range(NT)]
            kn32 = [sb(nc, es1, f"kn32_{i}", [128, 512], F32) for i in range(NT)]
            be = [kb.buf(f"etmp{i}") for i in range(NT)]
            dec = sb(nc, es1, "dec", [128, 4, 8], F32)
            bdec = kb.buf("dec")
            kteT = sb(nc, es1, "kteT", [128, 4, TG], BF16)
            bkte = kb.buf("kteT")
            Vg = sb(nc, es1, "Vg", [128, 4, D], BF16)
            bVg = kb.buf("Vg")
            kte = [sb(nc, es1, f"kte{i}", [128, 128], BF16) for i in range(2)]
            bktet = [kb.buf(f"kte{i}") for i in range(2)]
            r1 = sb(nc, es1, "r1", [128, 1], F32)
            br1 = kb.buf("r1")
            if not state_only:
                qf = sb(nc, es1, "qf", [128, 4, TG], BF16)
                qn = sb(nc, es1, "qn", [128, 4, TG], BF16)
                kng = sb(nc, es1, "kng", [128, 4, TG], BF16)
                kp = sb(nc, es1, "kp", [128, 4, TG], BF16)
                bqk = kb.buf("qk")
                sgT = sb(nc, es1, "sgT", [128, NCH, TG], BF16)
                bsg = kb.buf("sgT")
                AT = [sb(nc, es1, f"AT{i}", [128, 128], BF16) for i in range(4)]
                bAT = [kb.buf(f"AT{i}") for i in range(4)]
                a1 = [sb(nc, es1, f"a1_{i}", [128, 128], F32) for i in range(2)]
                a2 = [sb(nc, es1, f"a2_{i}", [128, 128], F32) for i in range(2)]
                ba = [kb.buf(f"a12_{i}") for i in range(2)]
                osq = [sb(nc, es1, f"osq{i}", [128, 2, 128], BF16) for i in range(2)]
                bosq = [kb.buf(f"osq{i}") for i in range(2)]
                rstd = [sb(nc, es1, f"rstd{i}", [128, 128], F32) for i in range(2)]
                brs = [kb.buf(f"rstd{i}") for i in range(2)]
                t1 = [sb(nc, es1, f"t1_{i}", [128, 128], F32) for i in range(2)]
                bt1 = [kb.buf(f"t1_{i}") for i in range(2)]

            xsrc = io["xT"].rearrange("(c p) t -> p c t", p=128)
            nb = 0
            bsc = [cm.bank[2], cm.bank[2]]
            if state_only:
                bds = [[cm.bank[3], cm.bank[4]], [cm.bank[2], cm.bank[5]]]
                ds_col = [[1536, 2048], [1024, 2560]]
            else:
                bds = [[cm.bank[3], cm.bank[4]], [cm.bank[3], cm.bank[4]]]
                ds_col = [[1536, 2048], [1536 + 256, 2048 + 256]]
            bo = [cm.bank[0], cm.bank[1]]
            bms = [cm.bank[5], cm.bank[5]]
            if not state_only:
                o_sb = [sb(nc, es1, f"o_sb{i}", [128, 256], F32) for i in range(3)]
                bosb = [kb.buf(f"o_sb{i}") for i in range(3)]

            def proj(lhs_cols, ug, bug, M=128):
                nonlocal nb
                bk = nb % 2
                nb += 1
                kb._pre(kb.pe, [bw, bug], [cm.bank[bk]])
                for c in range(NCH):
                    ins = nc.tensor.matmul(cm.ps[0:M, bk * 512:(bk + 1) * 512], w_in[:, c, lhs_cols], ug[:, c, :],
                                           start=(c == 0), stop=(c == NCH - 1))
                tok = kb.pe.mark(ins)
                kb._post(tok, [bw, bug], [cm.bank[bk]])
                return bk

            it = 0
            et = 0
            for g in range(NG):
                ug, bug = uTg[g % 2], bu[g % 2]
                for hf in range(2):
                    s_ = (2 * g + hf) % 2
                    tsl = slice(g * TG + hf * 256, g * TG + (hf + 1) * 256)
                    kb.dma(kb.sp, xs[s_][:], xsrc[:, :, tsl], writes=[xsb[s_]])
                    for c in range(NCH):
                        if c % 2 == 0:
                            kb.op(kb.act, lambda c=c, s_=s_, hf=hf: nc.scalar.activation(
                                out=ug[:, c, hf * 256:(hf + 1) * 256], in_=xs[s_][:, c, :], func=AF.Identity,
                                scale=cm.der[:, c:c + 1], bias=cm.der[:, 8 + c:9 + c]), [xsb[s_], B], [bug])
                        else:
                            kb.op(kb.dve, lambda c=c, s_=s_, hf=hf: v.tensor_scalar(
                                out=ug[:, c, hf * 256:(hf + 1) * 256], in0=xs[s_][:, c, :],
                                scalar1=cm.der[:, c:c + 1], scalar2=cm.der[:, 8 + c:9 + c], op0=ALU.mult, op1=ALU.add),
                                [xsb[s_], B], [bug])
                bk = proj(slice(3072, 3088), ug, bug, M=16)
                kb.op(kb.act, lambda bk=bk: nc.scalar.copy(out=gkT[:], in_=cm.ps[0:16, bk * 512:(bk + 1) * 512]), [cm.bank[bk]], [bgk])
                for h in range(4):
                    e = et % NT
                    et += 1
                    bk = nb % 2
                    nb += 1
                    kb._pre(kb.pe, [bw, bgk], [cm.bank[bk]])
                    ins = nc.tensor.matmul(cm.bank_ap(bk), wgk2[:, h * 128:(h + 1) * 128], gkT[:], start=True, stop=True)
                    tok = kb.pe.mark(ins)
                    kb._post(tok, [bw, bgk], [cm.bank[bk]])
                    kb.op(kb.act, lambda bk=bk, e=e, h=h: nc.scalar.activation(out=e1[e][:], in_=cm.bank_ap(bk), func=AF.Exp,
                                                                                  scale=-1.0, bias=gsm[:, h:h + 1]), [cm.bank[bk], B], [be[e]])
                    kb.op(kb.act, lambda e=e: nc.scalar.activation(out=e1[e][:], in_=e1[e][:], func=AF.Ln, scale=1.0, bias=1.0), [be[e]], [be[e]])
                    kb.op(kb.dve, lambda e=e: v.tensor_tensor_scan(out=ncum[e][:], data0=cmask[:], data1=e1[e][:], initial=0.0,
                                                                   op0=ALU.mult, op1=ALU.add), [be[e], B], [be[e]])
                    kb.op(kb.act, lambda e=e: nc.scalar.activation(out=epos[e][:], in_=ncum[e][:], func=AF.Exp, scale=-1.0 / 16.0), [be[e]], [be[e]])
                    kb.op(kb.act, lambda e=e: nc.scalar.activation(out=eneg[e][:], in_=ncum[e][:], func=AF.Exp, scale=1.0 / 16.0), [be[e]], [be[e]])
                    kb.op(kb.dve, lambda e=e, h=h: v.tensor_copy(out=dec[:, h, :], in_=epos[e][:, 63::64]), [be[e]], [bdec])
                    if state_only:
                        kb.op(kb.dve, lambda e=e: v.reduce_sum(out=r1[:], in_=ncum[e][:, 63::64], axis=AX.X), [be[e]], [br1])
                        kb.op(kb.dve, lambda h=h: v.tensor_tensor(out=nsum[:, h:h + 1], in0=nsum[:, h:h + 1], in1=r1[:], op=ALU.add), [br1, bns], [bns])
                    bk = proj(slice(512 + h * 128, 512 + (h + 1) * 128), ug, bug)
                    kb.op(kb.dve, lambda bk=bk, e=e: v.tensor_tensor(out=kn32[e][:], in0=cm.bank_ap(bk), in1=eneg[e][:], op=ALU.mult),
                          [cm.bank[bk], be[e]], [be[e]])
                    for n in range(8):
                        kb.op(kb.dve, lambda n=n, e=e, h=h: v.tensor_scalar(out=kteT[:, h, n * 64:(n + 1) * 64], in0=kn32[e][:, n * 64:(n + 1) * 64],
                                                                         scalar1=dec[:, h, n:n + 1], scalar2=None, op0=ALU.mult),
                              [be[e], bdec], [bkte])
                    if not state_only:
                        kb.op(kb.act, lambda e=e, h=h: nc.scalar.copy(out=kng[:, h, :], in_=kn32[e][:]), [be[e]], [bqk])
                        kb.op(kb.dve, lambda bk=bk, e=e, h=h: v.tensor_tensor(out=kp[:, h, :], in0=cm.bank_ap(bk), in1=epos[e][:], op=ALU.mult),
                              [cm.bank[bk], be[e]], [bqk])
                        bk = proj(slice(h * 128, (h + 1) * 128), ug, bug)
                        kb.op(kb.dve, lambda bk=bk, e=e, h=h: v.scalar_tensor_tensor(out=qf[:, h, :], in0=cm.bank_ap(bk), scalar=QS, in1=epos[e][:],
                                                                                      op0=ALU.mult, op1=ALU.mult), [cm.bank[bk], be[e]], [bqk])
                        kb.op(kb.dve, lambda bk=bk, e=e, h=h: v.scalar_tensor_tensor(out=qn[:, h, :], in0=cm.bank_ap(bk), scalar=QS, in1=eneg[e][:],
                                                                                      op0=ALU.mult, op1=ALU.mult), [cm.bank[bk], be[e]], [bqk])
                for tt in range(4):
                    for half in range(2):
                        bk = nb % 2
                        nb += 1
                        kb._pre(kb.pe, [bw, bug], [cm.bank[bk]])
                        for c in range(NCH):
                            ins = nc.tensor.matmul(cm.bank_ap(bk), ug[:, c, tt * 128:(tt + 1) * 128],
                                                   w_in[:, c, 1024 + half * 512:1024 + (half + 1) * 512], start=(c == 0), stop=(c == NCH - 1))
                        tok = kb.pe.mark(ins)
                        kb._post(tok, [bw, bug], [cm.bank[bk]])
                        kb.op(kb.act, lambda bk=bk, tt=tt, half=half: nc.scalar.copy(out=Vg[:, tt, half * 512:(half + 1) * 512], in_=cm.bank_ap(bk)),
                              [cm.bank[bk]], [bVg])
                if not state_only:
                    for vc in range(8):
                        bk = proj(slice(2048 + vc * 128, 2048 + (vc + 1) * 128), ug, bug)
                        kb.op(kb.act, lambda bk=bk, vc=vc: nc.scalar.activation(out=sgT[:, vc, :], in_=cm.bank_ap(bk), func=AF.Silu),
                              [cm.bank[bk]], [bsg])
                def p1(n):
                    tt, h = n // 4, n % 4
                    tl = slice(tt * 128, (tt + 1) * 128)
                    k = n % 2
                    kb._pre(kb.pe, [bkte, B], [cm.bbank[k]])
                    ins = nc.tensor.transpose(cm.psb[:, k * 1024:k * 1024 + 128], kteT[:, h, tl], cm.ident[:])
                    tok = kb.pe.mark(ins)
                    kb._post(tok, [bkte], [cm.bbank[k]])
                    if not state_only:
                        sc = bsc[k]
                        kb._pre(kb.pe, [bqk], [sc])
                        nc.tensor.matmul(cm.ps[:, 1024 + k * 256:1024 + k * 256 + 128], kng[:, h, tl], qf[:, h, tl], start=True, stop=True)
                        ins = nc.tensor.matmul(cm.ps[:, 1024 + k * 256 + 128:1024 + (k + 1) * 256], kp[:, h, tl], qn[:, h, tl], start=True, stop=True)
                        tok = kb.pe.mark(ins)
                        kb._post(tok, [bqk], [sc])

                def p2(n):
                    k = n % 2
                    kb.op(kb.act, lambda: nc.scalar.copy(out=kte[k][:], in_=cm.psb[:, k * 1024:k * 1024 + 128]), [cm.bbank[k]], [bktet[k]])
                    if not state_only:
                        sc = bsc[k]
                        q = n % 4
                        kb.op(kb.dve, lambda: v.tensor_tensor(out=a1[k][:], in0=cm.ps[:, 1024 + k * 256:1024 + k * 256 + 128], in1=M1[:], op=ALU.mult),
                              [sc, B], [ba[k]])
                        kb.op(kb.dve, lambda: v.tensor_tensor(out=a2[k][:], in0=cm.ps[:, 1024 + k * 256 + 128:1024 + (k + 1) * 256], in1=M2[:], op=ALU.mult),
                              [sc, B], [ba[k]])
                        kb.op(kb.dve, lambda: v.tensor_tensor(out=AT[q][:], in0=a1[k][:], in1=a2[k][:], op=ALU.add), [ba[k]], [bAT[q]])

                def p3(n):
                    tt, h = n // 4, n % 4
                    k = n % 2
                    for ch in range(2):
                        rows = slice(ch * 64, (ch + 1) * 64)
                        bd = bds[k][ch]
                        c0 = ds_col[k][ch]
                        kb._pre(kb.pe, [bktet[k], bVg], [bd])
                        ins = nc.tensor.matmul(cm.ps[:, c0:c0 + 256], kte[k][rows, :], Vg[rows, tt, h * 256:(h + 1) * 256], start=True, stop=True)
                        tok = kb.pe.mark(ins)
                        kb._post(tok, [bktet[k], bVg], [bd])

                def p4(n):
                    tt, h = n // 4, n % 4
                    k = n % 2
                    par = (g * 4 + tt) % 2
                    nn = tt * 2
                    ca0 = ds_col[k][0]
                    cb0 = ds_col[k][1]
                    kb.op(kb.dve, lambda: v.scalar_tensor_tensor(out=Sa[:, h, :], in0=S[:, h, :], scalar=dec[:, h, nn:nn + 1],
                                                                  in1=cm.ps[:, ca0:ca0 + 256], op0=ALU.mult, op1=ALU.add),
                          [bds[k][0], bS[h], bdec], [bSa[h]])
                    kb.op(kb.dve, lambda: v.scalar_tensor_tensor(out=S[:, h, :], in0=Sa[:, h, :], scalar=dec[:, h, nn + 1:nn + 2],
                                                                  in1=cm.ps[:, cb0:cb0 + 256], op0=ALU.mult, op1=ALU.add),
                          [bds[k][1], bSa[h], bdec], [bS[h]])
                    if not state_only:
                        kb.op(kb.act, lambda: nc.scalar.copy(out=Sbf[:, h, par * 2, :], in_=Sa[:, h, :]), [bSa[h]], [bSbf[h][par][0]])
                        kb.op(kb.pool, lambda: nc.gpsimd.tensor_copy(out=Sbf[:, h, par * 2 + 1, :], in_=S[:, h, :]), [bS[h]], [bSbf[h][par][1]])

                def p5(n):
                    tt, h = n // 4, n % 4
                    q = n % 4
                    par = (g * 4 + tt) % 2
                    bo_ = bo[n % 2]
                    o0 = (n % 2) * 512
                    prev_b, cur_a = bSbf[h][1 - par][1], bSbf[h][par][0]
                    kb._pre(kb.pe, [bAT[q], bVg, prev_b, cur_a, bqk], [bo_])
                    for vc in range(2):
                        nc.tensor.matmul(cm.ps[:, o0 + vc * 128:o0 + (vc + 1) * 128], Vg[:, tt, h * 256 + vc * 128:h * 256 + (vc + 1) * 128],
                                         AT[q][:], start=(vc == 0), stop=False, skip_group_check=True)
                    for ch in range(2):
                        sidx = (1 - par) * 2 + 1 if ch == 0 else par * 2
                        for vc in range(2):
                            ins = nc.tensor.matmul(cm.ps[:, o0 + vc * 128 + ch * 64:o0 + vc * 128 + (ch + 1) * 64],
                                                   Sbf[:, h, sidx, vc * 128:(vc + 1) * 128], qf[:, h, tt * 128 + ch * 64:tt * 128 + (ch + 1) * 64],
                                                   start=False, stop=(ch == 1), skip_group_check=True)
                    tok = kb.pe.mark(ins)
                    kb._post(tok, [bAT[q], bVg, prev_b, cur_a, bqk], [bo_])

                def p6(n):
                    k = n % 2
                    o0 = k * 512
                    kb.op(kb.act, lambda: nc.scalar.activation(out=osq[k][:], in_=cm.ps[:, o0:o0 + 256].rearrange("p (a b) -> p a b", a=2),
                                                               func=AF.Square), [bo[k]], [bosq[k]])
                    kb.op(kb.act, lambda: nc.scalar.copy(out=o_sb[n % 3][:], in_=cm.ps[:, o0:o0 + 256]), [bo[k]], [bosb[n % 3]])

                def p7(n):
                    k = n % 2
                    bm_ = bms[k]
                    m0 = 2560 + k * 128
                    kb._pre(kb.pe, [bosq[k], B], [bm_])
                    nc.tensor.matmul(cm.ps[:, m0:m0 + 128], ones256[:], osq[k][:, 0, :], start=True, stop=False)
                    ins = nc.tensor.matmul(cm.ps[:, m0:m0 + 128], ones256[:], osq[k][:, 1, :], start=False, stop=True)
                    tok = kb.pe.mark(ins)
                    kb._post(tok, [bosq[k]], [bm_])

                def p8(n):
                    tt, h = n // 4, n % 4
                    k = n % 2
                    tl = slice(tt * 128, (tt + 1) * 128)
                    m0 = 2560 + k * 128
                    kb.op(kb.act, lambda: nc.scalar.activation(out=rstd[k][:], in_=cm.ps[:, m0:m0 + 128], func=AF.Ln, bias=eps_r[:, 0:1], scale=1.0),
                          [bms[k], B], [brs[k]])
                    kb.op(kb.act, lambda: nc.scalar.activation(out=rstd[k][:], in_=rstd[k][:], func=AF.Exp, scale=-0.5), [brs[k]], [brs[k]])
                    for vc in range(2):
                        col = h * 2 + vc
                        kb.op(kb.dve, lambda vc=vc: v.tensor_tensor(out=t1[vc][:], in0=o_sb[n % 3][:, vc * 128:(vc + 1) * 128], in1=rstd[k][:],
                                                                    op=ALU.mult), [bosb[n % 3], brs[k]], [bt1[vc]])
                        kb.op(kb.pool, lambda vc=vc, col=col: nc.gpsimd.tensor_tensor(
                            out=mixT[:, col, g * TG + tl.start:g * TG + tl.stop], in0=t1[vc][:], in1=sgT[:, col, tl], op=ALU.mult),
                            [bt1[vc], bsg], [bmix])

                stages = [p1, p2, p3, p4] if state_only else [p1, p2, p3, p4, p5, p6, p7, p8]
                for step in range(16 + len(stages) - 1):
                    for off, fn in enumerate(stages):
                        n = step - off
                        if 0 <= n < 16:
                            fn(n)
            if state_only:
                for h in range(4):
                    kb.dma(kb.sp, io["sd_in"][:, h * 256:(h + 1) * 256], S[:, h, :], reads=[bS[h]], is_output=True)
                kb.op(kb.act, lambda: nc.scalar.activation(out=nsum[:, 4:8], in_=nsum[:, 0:4], func=AF.Exp, scale=-1.0 / 16.0), [bns], [bns])
                kb.dma(kb.sp, io["sd_in"][:, 1024:1028], nsum[:, 4:8], reads=[bns], is_output=True)
        if not state_only and not mixer_only:
            kb.barrier()
            with ExitStack() as es4:
                emit_outproj_ln_ffn(cm, es4, mixT, bmix, io["xT"], 0, io, row_scale=io["gsm"][:, 4:12])
    kb.barrier()


GROUPS = [[0, 1, 2, 3], [4, 5, 6, 7]]


def emit_mods(kb, cm, io, es):
    nc = kb.nc
    v = nc.vector
    c_sb = sb(nc, es, "c_sb", [128, NCH], F32)
    ca = sb(nc, es, "ca", [128, NCH], BF16)
    bA = sb(nc, es, "bA", [128, 48], F32)
    res = sb(nc, es, "mres", [128, 48], F32)
    wl = [sb(nc, es, f"wl{i}", [128, NCH, 1536], BF16) for i in range(2)]
    bc, bwl, br = kb.buf("c"), [kb.buf(f"wl{i}") for i in range(2)], kb.buf("mres")
    kb.dma(kb.sp, c_sb[:], io["c_pc"], writes=[bc])
    kb.dma(kb.sp, bA[:], io["bA_pc"], writes=[bc])
    kb.op(kb.act, lambda: nc.scalar.activation(out=ca[:], in_=c_sb[:], func=AF.Silu), [bc], [bc])
    wv = io["wA"].rearrange("(c p) f -> p c f", p=128)
    for l in range(DEPTH):
        s_ = l % 2
        kb.dma(kb.pool, wl[s_][:], wv[:, :, l * 1536:(l + 1) * 1536], writes=[bwl[s_]])
        bk = cm.bank[l % 2]
        kb._pre(kb.pe, [bc, bwl[s_]], [bk])
        for jj in range(12):
            for c in range(NCH):
                ins = nc.tensor.matmul(cm.ps[:, (l % 2) * 512 + jj:(l % 2) * 512 + jj + 1], wl[s_][:, c, jj * 128:(jj + 1) * 128], ca[:, c:c + 1],
                                       start=(c == 0), stop=(c == NCH - 1))
        tok = kb.pe.mark(ins)
        kb._post(tok, [bc, bwl[s_]], [bk])
        kb.op(kb.dve, lambda l=l: v.tensor_tensor(out=res[:, l * 12:(l + 1) * 12], in0=cm.ps[:, (l % 2) * 512:(l % 2) * 512 + 12],
                                                  in1=bA[:, l * 12:(l + 1) * 12], op=ALU.add), [bk, bc], [br])
    kb.dma(kb.sp, io["mods_in"], res[:], reads=[br])


def build_fused():
    kb = KB()
    nc = kb.nc
    io = {}

    def inp(name, shape, dt=F32):
        io[name] = nc.dram_tensor(name, shape, dt, kind="ExternalInput").ap()

    def scratch(name, shape, dt=F32):
        io[name] = nc.dram_tensor(name, shape, dt, kind="Internal").ap()

    inp("xT", [D, T])
    inp("c_pc", [128, NCH])
    inp("wA", [D, 6144])
    inp("bA_pc", [128, 48])
    inp("sel", [128, 4])
    inp("hb", [128, HALO])
    inp("gm", [128, 4])
    inp("gm1", [128, 4])
    for nm in ("onesD", "ident", "M1", "M2", "ones256"):
        inp(nm, [128, 128])
    inp("cmask", [128, 512])
    for l in range(DEPTH):
        inp(f"lnp{l}", [128, 32])
        inp(f"ff_w1_{l}", [D, DFF])
        inp(f"ff_w2_{l}", [DFF, D])
    for j in range(2):
        inp(f"gw_in{j}", [D, GLA_W])
        inp(f"gw_gk2{j}", [16, 512])
        inp(f"gsm{j}", [128, 16])
        inp(f"gw_out{j}", [D, D])
        inp(f"aw_in{j}", [D, 3 * D])
        inp(f"ab_qk{j}", [128, 16])
        inp(f"abv_bc{j}", [128, D])
        inp(f"atiles{j}", [16, 128, 640])
        inp(f"ahalo{j}", [4, 16, 128, 640])
        inp(f"aw_out{j}", [D, D])
    io["xT_final"] = nc.dram_tensor("xT_final", [D, T], F32, kind="ExternalOutput").ap()
    scratch("mods_in", [128, 48])
    scratch("mods_g", [4 * 128, 48])
    scratch("sd_in", [128, 1028])
    scratch("sd_g", [4 * 128, 1028])
    for hh in range(2):
        scratch(f"halo_in{hh}", [D, 256])
        scratch(f"hgat{hh}", [4 * D, 256])
    scratch("xbuf0", [D, T])
    scratch("xbuf1", [D, T])

    with ExitStack() as es0:
        cm = Common(kb, es0, io)
        pre_esm = ExitStack()
        pre_mixT = sb(nc, pre_esm, "bigg", [128, NCH, T], BF16)
        pre_esw = ExitStack()
        pre_w_in = sb(nc, pre_esw, "w_in", [128, NCH, GLA_W], BF16)
        pre_wgk2 = sb(nc, pre_esw, "wgk2", [16, 512], BF16)
        pre_bw = kb.buf("w_in")
        with ExitStack() as esm:
            emit_mods(kb, cm, io, esm)
        wv0 = io["gw_in0"].rearrange("(c p) f -> p c f", p=128)
        for c in range(NCH):
            kb.dma(kb.pool, pre_w_in[:, c, :], wv0[:, c, :], writes=[pre_bw])
        kb.dma(kb.pool, pre_wgk2[:], io["gw_gk20"], writes=[pre_bw])
        kb.barrier()
        kb.collective(io["mods_in"], io["mods_g"], GROUPS)
        kb.barrier()
        x_cur = io["xT"]
        for l in range(DEPTH):
            j = l // 2
            msrc = io["mods_g"].rearrange("(r p) f -> p r f", p=128)[:, :, l * 12:(l + 1) * 12]
            cm.load_layer(msrc, io[f"lnp{l}"], mods_view=lambda m: m[:].rearrange("p (r f) -> p r f", r=4))
            x_out = io["xT_final"] if l == DEPTH - 1 else io[f"xbuf{l % 2}"]
            lio = {"xT": x_cur, "xT_out": x_out, "ff_w1": io[f"ff_w1_{l}"], "ff_w2": io[f"ff_w2_{l}"]}
            if l % 2 == 0:
                lio.update({"w_in": io[f"gw_in{j}"], "w_gk2": io[f"gw_gk2{j}"], "gsm": io[f"gsm{j}"], "cmask": io["cmask"],
                            "M1": io["M1"], "M2": io["M2"], "ones256": io["ones256"], "w_out": io[f"gw_out{j}"],
                            "sd_in": io["sd_in"], "sd_g": io["sd_g"], "gm": io["gm"], "gm1": io["gm1"]})
                if l == 0:
                    esm_, esw = pre_esm, pre_esw
                    lio["mixT_sb"] = pre_mixT
                    w_in_sb, wgk2_sb, bw_ = pre_w_in, pre_wgk2, pre_bw
                else:
                    esm_ = ExitStack()
                    lio["mixT_sb"] = sb(nc, esm_, "bigg", [128, NCH, T], BF16)
                    esw = ExitStack()
                    w_in_sb = sb(nc, esw, "w_in", [128, NCH, GLA_W], BF16)
                    wgk2_sb = sb(nc, esw, "wgk2", [16, 512], BF16)
                    bw_ = kb.buf("w_in")
                    wv_ = lio["w_in"].rearrange("(c p) f -> p c f", p=128)
                    for c in range(NCH):
                        kb.dma(kb.pool, w_in_sb[:, c, :], wv_[:, c, :], writes=[bw_])
                    kb.dma(kb.pool, wgk2_sb[:], lio["w_gk2"], writes=[bw_])
                with esw:
                    lio.update({"w_in_sb": w_in_sb, "wgk2_sb": wgk2_sb, "bw": bw_})
                    emit_gla(kb, cm, lio, True)
                    kb.collective(io["sd_in"], io["sd_g"], GROUPS)
                    kb.barrier()
                    emit_gla(kb, cm, lio, False, mixer_only=True)
                kb.barrier()
                emit_gla_tail(kb, cm, lio)
                esm_.close()
                for hh in range(2):
                    kb.dma(kb.sp, io[f"halo_in{hh}"], x_out[:, T - HALO + hh * 256:T - HALO + (hh + 1) * 256])
                kb.barrier()
                halo_toks = [kb.collective(io[f"halo_in{hh}"], io[f"hgat{hh}"], GROUPS, wait=False) for hh in range(2)]
            else:
                lio.update({"w_in": io[f"aw_in{j}"], "b_qk": io[f"ab_qk{j}"], "bv_bc": io[f"abv_bc{j}"], "bias_tiles": io[f"atiles{j}"],
                            "w_out": io[f"aw_out{j}"], "hgat0": io["hgat0"], "hgat1": io["hgat1"], "sel": io["sel"], "hb": io["hb"],
                            "bias_halo": io[f"ahalo{j}"],
                            "halo_toks": halo_toks})
                emit_att(kb, cm, lio)
            x_cur = x_out
        kb.finish()
    return nc


_PROGS = {}


def _prog(name, fn):
    if name not in _PROGS:
        _PROGS[name] = fn()
    return _PROGS[name]


def _run(nc, in_maps):
    res = run_bass_kernel_spmd(nc, in_maps, core_ids=list(range(8)))
    return res.results


def _pc(vec):
    return np.ascontiguousarray(np.asarray(vec, np.float32).reshape(-1, 128).T)


def consts():
    onesD = np.full((128, 128), 1.0 / D, np.float32)
    ident = np.eye(128, dtype=np.float32)
    return onesD, ident


def att_bias_tiles(rel_bias):
    t = np.arange(128)[:, None]
    j = np.arange(640)[None, :]
    dist = 512 + t - j
    idx = np.clip(dist, -128, 128) + 128
    qa = t // 64
    kc = j // 64
    valid = (kc >= qa) & (kc <= qa + 8)
    tiles = np.asarray(rel_bias, np.float32)[:, idx]
    tiles = np.where(valid[None], tiles, np.float32(NEG)).astype(np.float32)
    return np.ascontiguousarray(tiles)


def gla_consts():
    t = np.arange(128)
    same = (t[:, None] // 64) == (t[None, :] // 64)
    M1 = (same & (t[:, None] <= t[None, :])).astype(np.float32)
    M2 = (same & (t[:, None] > t[None, :])).astype(np.float32)
    cm = np.ones((128, 512), np.float32)
    cm[:, 0::64] = 0.0
    ones256 = np.full((128, 128), 1.0 / 256.0, np.float32)
    return M1, M2, cm, ones256


def kernel(x, c, w_ada, b_ada, ln_g, ln_b, gla_w_in, gla_w_gk2, gla_b_gk, gla_g_norm, gla_w_out,
           att_w_in, att_b_in, att_rel_bias, att_w_out, ff_w1, ff_w2):
    f = lambda a: np.ascontiguousarray(np.asarray(a, np.float32))
    x, c, w_ada, b_ada, ln_g, ln_b = f(x), f(c), f(w_ada), f(b_ada), f(ln_g), f(ln_b)
    nc = _prog("fused", build_fused)
    onesD, ident = consts()
    M1, M2, cmask, ones256 = gla_consts()
    shared = {"onesD": onesD, "ident": ident, "M1": M1, "M2": M2, "ones256": ones256, "cmask": cmask}
    for l in range(DEPTH):
        shared[f"lnp{l}"] = np.ascontiguousarray(np.concatenate([_pc(ln_g[l, 0]), _pc(ln_b[l, 0]), _pc(ln_g[l, 1]), _pc(ln_b[l, 1])], axis=1))
        shared[f"ff_w1_{l}"] = f(ff_w1[l])
        shared[f"ff_w2_{l}"] = f(ff_w2[l])
    for j in range(2):
        gsm = np.zeros((128, 16), np.float32)
        gsm[:, 0:4] = -_pc(gla_b_gk[j])
        gsm[:, 4:12] = _pc(np.asarray(gla_g_norm[j], np.float32).reshape(-1))
        b_in = np.asarray(att_b_in[j], np.float32)
        shared[f"gw_in{j}"] = f(gla_w_in[j])
        shared[f"gw_gk2{j}"] = f(gla_w_gk2[j])
        shared[f"gsm{j}"] = gsm
        shared[f"gw_out{j}"] = f(gla_w_out[j])
        shared[f"aw_in{j}"] = f(att_w_in[j])
        shared[f"ab_qk{j}"] = np.ascontiguousarray(np.concatenate([_pc(b_in[0:D]), _pc(b_in[D:2 * D])], axis=1))
        shared[f"abv_bc{j}"] = np.ascontiguousarray(np.broadcast_to(b_in[2 * D:3 * D][None, :], (128, D)))
        shared[f"atiles{j}"] = att_bias_tiles(att_rel_bias[j])
        shared[f"aw_out{j}"] = f(att_w_out[j])
    in_maps = []
    for cid in range(8):
        b, r = cid // 4, cid % 4
        m = dict(shared)
        m["xT"] = np.ascontiguousarray(x[b, r * T:(r + 1) * T, :].T)
        m["c_pc"] = _pc(c[b])
        m["wA"] = np.ascontiguousarray(np.concatenate([w_ada[l][:, r * 1536:(r + 1) * 1536] for l in range(DEPTH)], axis=1))
        m["bA_pc"] = np.ascontiguousarray(np.concatenate([_pc(b_ada[l][r * 1536:(r + 1) * 1536]) for l in range(DEPTH)], axis=1))
        sel = np.zeros((128, 4), np.float32)
        if r > 0:
            sel[:, r - 1] = 1.0
        m["sel"] = sel
        m["hb"] = np.full((128, HALO), NEG if r == 0 else 0.0, np.float32)
        for j in range(2):
            tl_ = shared[f"atiles{j}"]
            hv = np.empty((4,) + tl_.shape, np.float32)
            for i in range(4):
                hv[i] = tl_
                if r == 0:
                    hv[i][:, :, 0:HALO - 128 * i] = NEG
            m[f"ahalo{j}"] = hv
        gm = np.zeros((128, 4), np.float32)
        gm[:, 0:r] = 1.0
        m["gm"] = gm
        m["gm1"] = np.ascontiguousarray(1.0 - gm)
        in_maps.append(m)
    res = _run(nc, in_maps)
    out = np.stack([np.concatenate([res[b * 4 + r]["xT_final"].T for r in range(4)], axis=0) for b in range(2)])
    return np.ascontiguousarray(out.astype(np.float32))
```

```python
import numpy as np
from contextlib import ExitStack
import concourse.bass as bass
import concourse.mybir as mybir
from concourse.bass_utils import run_bass_kernel_spmd

F32 = mybir.dt.float32
BF16 = mybir.dt.bfloat16
ALU = mybir.AluOpType
AF = mybir.ActivationFunctionType
AX = mybir.AxisListType

D = 1024
NCH = 8
T = 2048
TG = 512
NG = T // TG
HALO = 512
TE = T + HALO
DEPTH = 4
DFF = 4096
ALPHA = (2.0 * DEPTH) ** 0.25
LN_EPS_EFF = 1e-5 / (ALPHA * ALPHA)
RMS_EPS = 1e-6
NEG = -30000.0
DEBUG = False
GLA_W = 3088


class Eng:
    def __init__(self, nc, eng, name):
        self.nc, self.eng, self.name = nc, eng, name
        self.sem = nc.alloc_semaphore("sem_" + name)
        self.cnt = 0
        self.seen = {}
        self.self_sync = name in ("act", "dve", "pool")

    def mark(self, ins):
        ins.then_inc(self.sem, 1)
        self.cnt += 1
        return (self, self.cnt)

    def wait(self, tok):
        if tok is None:
            return
        prod, val = tok
        if prod is self and not self.self_sync:
            return
        if self.seen.get(id(prod), 0) >= val:
            return
        self.eng.wait_ge(prod.sem, val)
        self.seen[id(prod)] = val


class DSem:
    def __init__(self, nc, name):
        self.sem = nc.alloc_semaphore(name)
        self.total = 0


class Buf:
    def __init__(self, name):
        self.name = name
        self.ready = None
        self.readers = {}


class KB:
    def __init__(self):
        self.nc = nc = bass.Bass("TRN2", target_bir_lowering=False)
        self.pe = Eng(nc, nc.tensor, "pe")
        self.act = Eng(nc, nc.scalar, "act")
        self.dve = Eng(nc, nc.vector, "dve")
        self.pool = Eng(nc, nc.gpsimd, "pool")
        self.sp = Eng(nc, nc.sync, "sp")
        self.dsems = {"sp": [DSem(nc, f"dsp{i}") for i in range(12)],
                      "pool": [DSem(nc, f"dpl{i}") for i in range(12)]}
        self.dnext = {"sp": 0, "pool": 0}
        self.ccsem = DSem(nc, "ccsem")
        self.out_toks = []
        self.bufs = {}

    def buf(self, name):
        b = Buf(name)
        return b

    def _pre(self, E, reads, writes):
        for b in reads:
            E.wait(b.ready)
        for b in writes:
            E.wait(b.ready)
            for t in list(b.readers.values()):
                E.wait(t)

    def _post(self, tok, reads, writes):
        for b in writes:
            b.ready = tok
            b.readers = {}
        for b in reads:
            b.readers[id(tok[0])] = tok

    def op(self, E, fn, reads=(), writes=(), mark=True):
        self._pre(E, reads, writes)
        ins = fn()
        if not mark:
            return None
        tok = E.mark(ins)
        self._post(tok, reads, writes)
        return tok

    def dma(self, E, out_ap, in_ap, reads=(), writes=(), is_output=False):
        self._pre(E, reads, writes)
        lst = self.dsems[E.name]
        ds = lst[self.dnext[E.name] % len(lst)]
        self.dnext[E.name] += 1
        if ds.total > 0:
            E.wait((ds, ds.total))
        ins = E.eng.dma_start(out=out_ap, in_=in_ap)
        ins.then_inc(ds.sem, 16)
        ds.total += 16
        tok = (ds, ds.total)
        self._post(tok, reads, writes)
        if is_output:
            self.out_toks.append(tok)
        return tok

    def collective(self, in_ap, out_ap, groups, wait=True):
        E = self.pool
        ins = self.nc.gpsimd.collective_compute("AllGather", ALU.bypass, replica_groups=groups, ins=[in_ap], outs=[out_ap])
        ins.then_inc(self.ccsem.sem, 1)
        self.ccsem.total += 1
        tok = (self.ccsem, self.ccsem.total)
        if wait:
            E.wait(tok)
        return tok

    def barrier(self):
        engs = [self.pe, self.act, self.dve, self.pool, self.sp]
        snap = [(P, P.cnt) for P in engs if P.cnt > 0]
        dsn = [(ds, ds.total) for lst in self.dsems.values() for ds in lst if ds.total > 0]
        if self.ccsem.total > 0:
            dsn.append((self.ccsem, self.ccsem.total))
        for E in engs:
            for tok in snap + dsn:
                E.wait(tok)

    def finish(self):
        for tok in self.out_toks:
            self.sp.wait(tok)
        for lst in self.dsems.values():
            for ds in lst:
                if ds.total > 0:
                    self.sp.wait((ds, ds.total))


_UID = [0]


def sb(nc, es, name, shape, dt):
    _UID[0] += 1
    return es.enter_context(nc.sbuf_tensor(f"s{_UID[0]}_" + name, shape, dt))


class Common:
    def __init__(self, kb, es, io):
        self.kb = kb
        nc = kb.nc
        self.nc = nc
        self.io = io
        self.ps = es.enter_context(nc.psum_tensor("ps", [128, 3072], F32))
        self.psb = es.enter_context(nc.psum_tensor("psb", [128, 2048], BF16))
        self.bank = [kb.buf(f"bank{i}") for i in range(6)]
        self.bbank = [kb.buf(f"bbank{i}") for i in range(2)]
        self.mods = sb(nc, es, "mods", [128, 48], F32)
        self.lnp = sb(nc, es, "lnp", [128, 32], F32)
        self.der = sb(nc, es, "der", [128, 64], F32)
        self.onesD = sb(nc, es, "onesD", [128, 128], BF16)
        self.ident = sb(nc, es, "ident", [128, 128], BF16)
        self.eps_ln = sb(nc, es, "eps_ln", [128, 1], F32)
        self.uid = 0
        self.b_small = kb.buf("small")
        kb.dma(kb.pool, self.onesD[:], io["onesD"], writes=[self.b_small])
        kb.dma(kb.pool, self.ident[:], io["ident"], writes=[self.b_small])
        if "mods_pc" in io:
            self.load_layer(io["mods_pc"], io["lnp"])

    def load_layer(self, mods_ap, lnp_ap, mods_view=None):
        kb, nc = self.kb, self.nc
        if mods_view is None:
            kb.dma(kb.sp, self.mods[:], mods_ap, writes=[self.b_small])
        else:
            kb.dma(kb.sp, mods_view(self.mods), mods_ap, writes=[self.b_small])
        kb.dma(kb.sp, self.lnp[:], lnp_ap, writes=[self.b_small])
        m, l, d = self.mods, self.lnp, self.der
        v = nc.vector
        B = [self.b_small]
        kb.op(kb.dve, lambda: v.tensor_scalar(out=d[:, 0:8], in0=m[:, 8:16], scalar1=1.0, scalar2=None, op0=ALU.add), B, B)
        kb.op(kb.dve, lambda: v.tensor_copy(out=d[:, 8:16], in_=m[:, 0:8]), B, B)
        kb.op(kb.dve, lambda: v.tensor_scalar(out=d[:, 16:24], in0=m[:, 16:24], scalar1=1.0, scalar2=1.0 / ALPHA, op0=ALU.add, op1=ALU.mult), B, B)
        kb.op(kb.dve, lambda: v.tensor_scalar(out=d[:, 48:56], in0=m[:, 32:40], scalar1=1.0, scalar2=None, op0=ALU.add), B, B)
        kb.op(kb.dve, lambda: v.tensor_tensor(out=d[:, 24:32], in0=d[:, 48:56], in1=l[:, 0:8], op=ALU.mult), B, B)
        kb.op(kb.dve, lambda: v.tensor_tensor(out=d[:, 32:40], in0=d[:, 48:56], in1=l[:, 8:16], op=ALU.mult), B, B)
        kb.op(kb.dve, lambda: v.tensor_tensor(out=d[:, 32:40], in0=d[:, 32:40], in1=m[:, 24:32], op=ALU.add), B, B)
        kb.op(kb.dve, lambda: v.tensor_scalar(out=d[:, 40:48], in0=m[:, 40:48], scalar1=1.0, scalar2=1.0 / ALPHA, op0=ALU.add, op1=ALU.mult), B, B)

    def bank_ap(self, i, n=512):
        return self.ps[:, i * 512:i * 512 + n]


LH = 256
NH = T // LH


def ln_pipeline(cm, xres, xbh, gcol, bcol, tmps, out_u=None, ubufs=None, acol=None, bcol2=None, out_dram=None, hooks=None):
    kb, nc = cm.kb, cm.nc
    v = nc.vector
    hooks = hooks or {}

    def sl(j):
        return slice(j * LH, (j + 1) * LH)

    def l1(j):
        t = tmps[j % 2]
        z = xres[:, :, sl(j)]
        kb.op(kb.act, lambda: nc.scalar.activation(out=t["zb"][:], in_=z, func=AF.Identity), [xbh[j]], [t["bzb"]])
        kb.op(kb.act, lambda: nc.scalar.activation(out=t["zsq"][:], in_=z, func=AF.Square), [xbh[j]], [t["bzsq"]])

    def l2(j):
        t = tmps[j % 2]
        bk = cm.bank[4 + j % 2]
        c0 = (4 + j % 2) * 512
        kb._pre(kb.pe, [t["bzb"], t["bzsq"], cm.b_small], [bk])
        for c in range(NCH):
            nc.tensor.matmul(cm.ps[:, c0:c0 + LH], cm.onesD[:], t["zb"][:, c, :], start=(c == 0), stop=(c == NCH - 1))
        for c in range(NCH):
            ins = nc.tensor.matmul(cm.ps[:, c0 + LH:c0 + 2 * LH], cm.onesD[:], t["zsq"][:, c, :], start=(c == 0), stop=(c == NCH - 1))
        tok = kb.pe.mark(ins)
        kb._post(tok, [t["bzb"], t["bzsq"]], [bk])

    def l3(j):
        t = tmps[j % 2]
        bk = cm.bank[4 + j % 2]
        c0 = (4 + j % 2) * 512
        st, bst = t["st"], t["bst"]
        mean, t2, rstd = st[:, 0, :], st[:, 1, :], st[:, 2, :]
        kb.op(kb.dve, lambda: v.tensor_copy(out=mean, in_=cm.ps[:, c0:c0 + LH]), [bk], [bst])
        kb.op(kb.dve, lambda: v.tensor_tensor(out=t2, in0=mean, in1=mean, op=ALU.mult), [bst], [bst])
        kb.op(kb.dve, lambda: v.tensor_tensor(out=t2, in0=cm.ps[:, c0 + LH:c0 + 2 * LH], in1=t2, op=ALU.subtract), [bk, bst], [bst])
        kb.op(kb.act, lambda: nc.scalar.activation(out=t2, in_=t2, func=AF.Ln, bias=cm.eps_ln[:, 0:1], scale=1.0), [bst, cm.b_small], [bst])
        kb.op(kb.act, lambda: nc.scalar.activation(out=rstd, in_=t2, func=AF.Exp, scale=-0.5), [bst], [bst])

    def l4(j):
        t = tmps[j % 2]
        st, bst = t["st"], t["bst"]
        z = xres[:, :, sl(j)]
        mb = st[:, 0, :].unsqueeze(1).broadcast_to([128, NCH, LH])
        rb = st[:, 2, :].unsqueeze(1).broadcast_to([128, NCH, LH])
        kb.op(kb.dve, lambda: v.tensor_tensor(out=z, in0=z, in1=mb, op=ALU.subtract), [xbh[j], bst], [xbh[j]])
        kb.op(kb.dve, lambda: v.tensor_tensor(out=z, in0=z, in1=rb, op=ALU.mult), [xbh[j], bst], [xbh[j]])

    def l5(j):
        if out_u is not None:
            for c in range(NCH):
                kb.op(kb.act, lambda c=c: nc.scalar.activation(
                    out=out_u[:, c, sl(j)], in_=xres[:, c, sl(j)], func=AF.Identity,
                    scale=cm.der[:, acol + c:acol + c + 1], bias=cm.der[:, bcol2 + c:bcol2 + c + 1]), [xbh[j], cm.b_small], [ubufs[j // 2]])
        for c in range(NCH):
            kb.op(kb.dve, lambda c=c: v.tensor_scalar(
                out=xres[:, c, sl(j)], in0=xres[:, c, sl(j)],
                scalar1=cm.lnp[:, gcol + c:gcol + c + 1], scalar2=cm.lnp[:, bcol + c:bcol + c + 1],
                op0=ALU.mult, op1=ALU.add), [xbh[j], cm.b_small], [xbh[j]])
        if out_dram is not None:
            kb.dma(kb.sp, out_dram[:, :, sl(j)], xres[:, :, sl(j)], reads=[xbh[j]], is_output=True)

    stages = [l1, l2, l3, l4, l5]
    for step in range(NH + len(stages) - 1):
        if step in hooks:
            hooks[step]()
        for off, fn in enumerate(stages):
            j = step - off
            if 0 <= j < NH:
                fn(j)


def emit_outproj_ln_ffn(cm, es, mixT, bmix, x_src, x_off, io=None, row_scale=None):
    kb, nc = cm.kb, cm.nc
    if io is None:
        io = cm.io
    v = nc.vector
    xres = sb(nc, es, "xres", [128, NCH, T], F32)
    xbh = [kb.buf(f"xres{j}") for j in range(NH)]
    xb2 = lambda g: [xbh[2 * g], xbh[2 * g + 1]]
    u2 = sb(nc, es, "u2", [128, NCH, T], BF16)
    u2b = [kb.buf(f"u2_{g}") for g in range(NG)]
    wslot = [sb(nc, es, f"wslot{i}", [128, 8192], BF16) for i in range(2)]
    wsb = [kb.buf(f"wslot{i}") for i in range(2)]
    w1s = [w[:, 0:4096].rearrange("p (c f) -> p c f", c=8) for w in wslot]
    w2s = [w[:, 4096:8192].rearrange("p (c f) -> p c f", c=4) for w in wslot]
    wos = wslot[0][:, :].rearrange("p (c f) -> p c f", c=8)
    hT = [sb(nc, es, f"hT{i}", [128, 4, TG], BF16) for i in range(2)]
    hTb = [kb.buf(f"hT{i}") for i in range(2)]
    rl = [sb(nc, es, f"rl{i}", [128, TG], BF16) for i in range(2)]
    rlb = [kb.buf(f"rl{i}") for i in range(2)]
    tmps = []
    for i_ in range(2):
        tmps.append({"zb": sb(nc, es, f"zb{i_}", [128, NCH, LH], BF16), "zsq": sb(nc, es, f"zsq{i_}", [128, NCH, LH], BF16),
                     "st": sb(nc, es, f"lnst{i_}", [128, 3, LH], F32),
                     "bzb": kb.buf(f"zb{i_}"), "bzsq": kb.buf(f"zsq{i_}"), "bst": kb.buf(f"lnst{i_}")})
    kb.op(kb.dve, lambda: v.memset(cm.eps_ln[:], LN_EPS_EFF), [], [cm.b_small])

    xsrc = x_src.rearrange("(c p) t -> p c t", p=128)
    for g in range(NG):
        kb.dma(kb.sp, xres[:, :, g * TG:(g + 1) * TG], xsrc[:, :, x_off + g * TG:x_off + (g + 1) * TG], writes=xb2(g))
    kb.dma(kb.pool, wos, io["w_out"].rearrange("(c p) f -> p c f", p=128), writes=[wsb[0]])
    if row_scale is not None:
        rs = sb(nc, es, "rowscale", [128, NCH], F32)
        kb.dma(kb.sp, rs[:], row_scale, writes=[cm.b_small])
        for c in range(NCH):
            kb.op(kb.dve, lambda c=c: v.tensor_scalar(out=wos[:, c, :], in0=wos[:, c, :], scalar1=rs[:, c:c + 1], scalar2=None, op0=ALU.mult),
                  [wsb[0], cm.b_small], [wsb[0]])
    w1v = io["ff_w1"].rearrange("(c p) f -> p c f", p=128)
    w2v = io["ff_w2"].rearrange("(e c p) d -> e p c d", p=128, c=4)
    NE = 8

    def load_ffn_e(e, s):
        kb.dma(kb.pool, w1s[s], w1v[:, :, e * 512:(e + 1) * 512], writes=[wsb[s]])
        kb.dma(kb.pool, w2s[s], w2v[e], writes=[wsb[s]])

    load_ffn_e(0, 1)
    nb = 0

    def outproj(g):
        nonlocal nb
        for dc in range(NCH):
            bk = nb % 4
            nb += 1
            kb._pre(kb.pe, [wsb[0], bmix], [cm.bank[bk]])
            for c in range(NCH):
                ins = nc.tensor.matmul(cm.bank_ap(bk), wos[:, c, dc * 128:(dc + 1) * 128],
                                       mixT[:, c, g * TG:(g + 1) * TG], start=(c == 0), stop=(c == NCH - 1))
            tok = kb.pe.mark(ins)
            kb._post(tok, [wsb[0], bmix], [cm.bank[bk]])
            kb.op(kb.dve, lambda dc=dc, bk=bk, g=g: v.scalar_tensor_tensor(
                out=xres[:, dc, g * TG:(g + 1) * TG], in0=cm.bank_ap(bk), scalar=cm.der[:, 16 + dc:17 + dc],
                in1=xres[:, dc, g * TG:(g + 1) * TG], op0=ALU.mult, op1=ALU.add), [cm.bank[bk], cm.b_small] + xb2(g), xb2(g))

    steps = [(e, g) for e in range(NE) for g in range(NG)]

    def emit_H(si):
        nonlocal nb
        e, g = steps[si]
        s = (e + 1) % 2
        hs = si % 2
        for fc in range(4):
            bk = nb % 4
            nb += 1
            kb._pre(kb.pe, [wsb[s], u2b[g]], [cm.bank[bk]])
            for c in range(NCH):
                ins = nc.tensor.matmul(cm.bank_ap(bk), w1s[s][:, c, fc * 128:(fc + 1) * 128],
                                       u2[:, c, g * TG:(g + 1) * TG], start=(c == 0), stop=(c == NCH - 1))
            tok = kb.pe.mark(ins)
            kb._post(tok, [wsb[s], u2b[g]], [cm.bank[bk]])
            r = fc % 2
            kb.op(kb.act, lambda bk=bk, r=r: nc.scalar.activation(out=rl[r][:], in_=cm.bank_ap(bk), func=AF.Relu),
                  [cm.bank[bk]], [rlb[r]])
            kb.op(kb.dve, lambda fc=fc, r=r, hs=hs: v.tensor_tensor(out=hT[hs][:, fc, :], in0=rl[r][:], in1=rl[r][:], op=ALU.mult),
                  [rlb[r]], [hTb[hs]])

    def emit_Y(si):
        nonlocal nb
        e, g = steps[si]
        s = (e + 1) % 2
        hs = si % 2
        for dc in range(NCH):
            bk = nb % 4
            nb += 1
            kb._pre(kb.pe, [wsb[s], hTb[hs]], [cm.bank[bk]])
            for fc in range(4):
                ins = nc.tensor.matmul(cm.bank_ap(bk), w2s[s][:, fc, dc * 128:(dc + 1) * 128],
                                       hT[hs][:, fc, :], start=(fc == 0), stop=(fc == 3))
            tok = kb.pe.mark(ins)
            kb._post(tok, [wsb[s], hTb[hs]], [cm.bank[bk]])
            kb.op(kb.dve, lambda dc=dc, bk=bk, g=g: v.scalar_tensor_tensor(
                out=xres[:, dc, g * TG:(g + 1) * TG], in0=cm.bank_ap(bk), scalar=cm.der[:, 40 + dc:41 + dc],
                in1=xres[:, dc, g * TG:(g + 1) * TG], op0=ALU.mult, op1=ALU.add), [cm.bank[bk], cm.b_small] + xb2(g), xb2(g))

    xo = io["xT_out"].rearrange("(c p) t -> p c t", p=128)

    def ffn_seq():
        emit_H(0)
        load_ffn_e(1, 0)
        yield
        for si in range(len(steps)):
            e, g = steps[si]
            if si + 1 < len(steps):
                emit_H(si + 1)
            emit_Y(si)
            if g == NG - 1 and e + 2 < NE:
                load_ffn_e(e + 2, (e + 1) % 2)
            yield

    ffn = ffn_seq()
    outproj(0)
    hooks = {0: lambda: outproj(1), 2: lambda: outproj(2), 4: lambda: outproj(3)}
    for st_ in (6, 8, 10):
        hooks[st_] = lambda: next(ffn)
    ln_pipeline(cm, xres, xbh, 0, 8, tmps, out_u=u2, ubufs=u2b, acol=24, bcol2=32, hooks=hooks)
    n_done = 3
    while n_done < 1 + len(steps) - 3:
        next(ffn)
        n_done += 1
    hooks2 = {0: lambda: next(ffn), 2: lambda: next(ffn), 4: lambda: next(ffn)}
    ln_pipeline(cm, xres, xbh, 16, 24, tmps, out_dram=xo, hooks=hooks2)
    for _ in ffn:
        pass


def emit_att(kb, cm, io):
    nc = kb.nc
    with ExitStack() as es0:
        v = nc.vector
        big = sb(nc, es0, "big", [128, NCH, TE], BF16)
        mixT = big[:, :, 0:T]
        bmix = kb.buf("mixT")
        with ExitStack() as es1:
            bqk = sb(nc, es1, "bqk", [128, 16], F32)
            bvb = sb(nc, es1, "bvb", [128, D], F32)
            hb = sb(nc, es1, "hb", [128, HALO], F32)
            kb.dma(kb.sp, bqk[:], io["b_qk"], writes=[cm.b_small])
            kb.dma(kb.sp, bvb[:], io["bv_bc"], writes=[cm.b_small])
            kb.dma(kb.sp, hb[:], io["hb"], writes=[cm.b_small])
            kb.op(kb.dve, lambda: v.tensor_scalar(out=bqk[:, 0:8], in0=bqk[:, 0:8], scalar1=0.125, scalar2=None, op0=ALU.mult),
                  [cm.b_small], [cm.b_small])
            QT = sb(nc, es1, "QT", [128, NCH, T], BF16)
            KT = sb(nc, es1, "KT", [128, NCH, TE], BF16)
            V = sb(nc, es1, "V", [128, TE // 128, D], BF16)
            bQ, bK, bV = kb.buf("QT"), kb.buf("KT"), kb.buf("V")
            with ExitStack() as es2:
                uT = big
                ub = [kb.buf(f"uT{g}") for g in range(TE // TG)]
                xs = [sb(nc, es2, f"xs{i}", [128, NCH, 256], F32) for i in range(2)]
                xsb = [kb.buf(f"xs{i}") for i in range(2)]
                wsl = [sb(nc, es2, f"wsl{i}", [128, NCH, 512], BF16) for i in range(2)]
                wslb = [kb.buf(f"wsl{i}") for i in range(2)]
                xsrc = io["xT"].rearrange("(c p) t -> p c t", p=128)
                hsrc = [io[f"hgat{hh}"].rearrange("(j c p) t -> j p c t", p=128, c=NCH) for hh in range(2)]
                w_in_v = io["w_in"].rearrange("(c p) f -> p c f", p=128)
                xh = sb(nc, es2, "xh", [128, NCH, 256], F32)
                xhb = kb.buf("xh")
                sel = sb(nc, es2, "sel", [128, 4], F32)
                kb.dma(kb.sp, sel[:], io["sel"], writes=[cm.b_small])
                for wb in range(2):
                    kb.dma(kb.pool, wsl[wb][:], w_in_v[:, :, wb * 512:(wb + 1) * 512], writes=[wslb[wb]])
                bhalo = [kb.buf("hgat0"), kb.buf("hgat1")]
                for hh in range(2):
                    bhalo[hh].ready = io["halo_toks"][hh]
                for cnt_, hg in enumerate(list(range(2, TE // 256)) + [0, 1]):
                    s = cnt_ % 2
                    g = hg // 2
                    tsl = slice(hg * 256, (hg + 1) * 256)
                    if hg < 2:
                        for j in range(4):
                            kb.dma(kb.sp, xh[:], hsrc[hg][j], reads=[bhalo[hg]], writes=[xhb])
                            if j == 0:
                                kb.op(kb.dve, lambda s=s: v.tensor_scalar(out=xs[s][:], in0=xh[:], scalar1=sel[:, 0:1], scalar2=None, op0=ALU.mult),
                                      [xhb, cm.b_small], [xsb[s]])
                            else:
                                kb.op(kb.dve, lambda s=s, j=j: v.scalar_tensor_tensor(out=xs[s][:], in0=xh[:], scalar=sel[:, j:j + 1], in1=xs[s][:],
                                                                                 op0=ALU.mult, op1=ALU.add), [xhb, xsb[s], cm.b_small], [xsb[s]])
                    else:
                        kb.dma(kb.sp, xs[s][:], xsrc[:, :, (hg - 2) * 256:(hg - 1) * 256], writes=[xsb[s]])
                    for c in range(NCH):
                        if c % 2 == 0:
                            kb.op(kb.act, lambda c=c, s=s, tsl=tsl: nc.scalar.activation(
                                out=uT[:, c, tsl], in_=xs[s][:, c, :], func=AF.Identity,
                                scale=cm.der[:, c:c + 1], bias=cm.der[:, 8 + c:9 + c]), [xsb[s], cm.b_small], [ub[g]])
                        else:
                            kb.op(kb.dve, lambda c=c, s=s, tsl=tsl: v.tensor_scalar(
                                out=uT[:, c, tsl], in0=xs[s][:, c, :],
                                scalar1=cm.der[:, c:c + 1], scalar2=cm.der[:, 8 + c:9 + c], op0=ALU.mult, op1=ALU.add),
                                [xsb[s], cm.b_small], [ub[g]])
                nb = 0
                for wb in range(6):
                    s = wb % 2
                    if wb < 4:
                        isq = wb < 2
                        groups = [1, 2, 3, 4] if isq else [1, 2, 3, 4, 0]
                        for cc in range(4):
                            oc = (wb % 2) * 4 + cc
                            for g in groups:
                                bk = nb % 4
                                nb += 1
                                kb._pre(kb.pe, [wslb[s], ub[g]], [cm.bank[bk]])
                                for c in range(NCH):
                                    ins = nc.tensor.matmul(cm.bank_ap(bk), wsl[s][:, c, cc * 128:(cc + 1) * 128],
                                                           uT[:, c, g * TG:(g + 1) * TG], start=(c == 0), stop=(c == NCH - 1))
                                tok = kb.pe.mark(ins)
                                kb._post(tok, [wslb[s], ub[g]], [cm.bank[bk]])
                                if isq:
                                    kb.op(kb.act, lambda bk=bk, oc=oc, g=g: nc.scalar.activation(
                                        out=QT[:, oc, (g - 1) * TG:g * TG], in_=cm.bank_ap(bk), func=AF.Identity,
                                        scale=0.125, bias=bqk[:, oc:oc + 1]), [cm.bank[bk], cm.b_small], [bQ])
                                else:
                                    kb.op(kb.act, lambda bk=bk, oc=oc, g=g: nc.scalar.activation(
                                        out=KT[:, oc, g * TG:(g + 1) * TG], in_=cm.bank_ap(bk), func=AF.Identity,
                                        scale=1.0, bias=bqk[:, 8 + oc:9 + oc]), [cm.bank[bk], cm.b_small], [bK])
                    else:
                        vb = wb - 4
                        for tt in list(range(4, TE // 128)) + [0, 1, 2, 3]:
                            g = tt // 4
                            bk = nb % 4
                            nb += 1
                            kb._pre(kb.pe, [wslb[s], ub[g]], [cm.bank[bk]])
                            for c in range(NCH):
                                ins = nc.tensor.matmul(cm.bank_ap(bk), uT[:, c, tt * 128:(tt + 1) * 128],
                                                       wsl[s][:, c, :], start=(c == 0), stop=(c == NCH - 1))
                            tok = kb.pe.mark(ins)
                            kb._post(tok, [wslb[s], ub[g]], [cm.bank[bk]])
                            kb.op(kb.dve, lambda bk=bk, tt=tt, vb=vb: v.tensor_tensor(
                                out=V[:, tt, vb * 512:(vb + 1) * 512], in0=cm.bank_ap(bk), in1=bvb[:, vb * 512:(vb + 1) * 512],
                                op=ALU.add), [cm.bank[bk], cm.b_small], [bV])
                    if wb + 2 < 6:
                        kb.dma(kb.pool, wsl[s][:], w_in_v[:, :, (wb + 2) * 512:(wb + 3) * 512], writes=[wslb[s]])
            kb.barrier()
            with ExitStack() as es3:
                bt = [sb(nc, es3, f"bt{i}", [128, 640], F32) for i in range(2)]
                btb = [kb.buf(f"bt{i}") for i in range(2)]
                ssb = [sb(nc, es3, f"ssb{i}", [128, 640], F32) for i in range(2)]
                ssbb = [kb.buf(f"ssb{i}") for i in range(2)]
                pbf = [sb(nc, es3, f"pbf{i}", [128, 640], BF16) for i in range(2)]
                pbfb = [kb.buf(f"pbf{i}") for i in range(2)]
                pT = [sb(nc, es3, f"pT{i}", [128, 640], BF16) for i in range(2)]
                pTb = [kb.buf(f"pT{i}") for i in range(2)]
                stat = [sb(nc, es3, f"stat{i}", [128, 4], F32) for i in range(2)]
                statb = [kb.buf(f"stat{i}") for i in range(2)]
                its = [(h, i) for h in range(16) for i in range(T // 128)]
                NIT = len(its)
                bt4 = bt + [sb(nc, es3, f"bt{q_}", [128, 640], F32) for q_ in (2, 3)]
                bt4b = btb + [kb.buf("bt2"), kb.buf("bt3")]

                def bslot(h, i):
                    return (h * 5 + min(i, 4)) % 4
                pbf3 = pbf + [sb(nc, es3, "pbf2", [128, 640], BF16)]
                pbf3b = pbfb + [kb.buf("pbf2")]
                stat4 = stat + [sb(nc, es3, f"stat{q_}", [128, 4], F32) for q_ in (2, 3)]
                stat4b = statb + [kb.buf("stat2"), kb.buf("stat3")]

                def st_qk(n):
                    h, i = its[n]
                    k, c, r0, bs = n % 2, h // 2, (h % 2) * 64, h % 2
                    if i <= 4:
                        sl_ = bslot(h, i)
                        src_ = io["bias_halo"][i, h] if i < 4 else io["bias_tiles"][h]
                        kb.dma(kb.sp, bt4[sl_][:], src_, writes=[bt4b[sl_]])
                    bkA, bkB = cm.bank[2 * k], cm.bank[2 * k + 1]
                    kb._pre(kb.pe, [bQ, bK], [bkA, bkB])
                    nc.tensor.matmul(cm.ps[:, 2 * k * 512:2 * k * 512 + 512], QT[r0:r0 + 64, c, i * 128:(i + 1) * 128],
                                     KT[r0:r0 + 64, c, i * 128:i * 128 + 512], start=True, stop=True)
                    ins = nc.tensor.matmul(cm.ps[:, 2 * k * 512 + 512:2 * k * 512 + 640], QT[r0:r0 + 64, c, i * 128:(i + 1) * 128],
                                           KT[r0:r0 + 64, c, i * 128 + 512:i * 128 + 640], start=True, stop=True)
                    tok = kb.pe.mark(ins)
                    kb._post(tok, [bQ, bK], [bkA, bkB])

                def st_front(n):
                    h, i = its[n]
                    k, bs, q4 = n % 2, h % 2, n % 4
                    bkA, bkB = cm.bank[2 * k], cm.bank[2 * k + 1]
                    sl_ = bslot(h, i)
                    kb.op(kb.dve, lambda: v.tensor_tensor(out=ssb[k][:], in0=cm.ps[:, 2 * k * 512:2 * k * 512 + 640],
                                                          in1=bt4[sl_][:], op=ALU.add), [bkA, bkB, bt4b[sl_]], [ssbb[k]])
                    kb.op(kb.dve, lambda: v.reduce_max(out=stat4[q4][:, 0:1], in_=ssb[k][:], axis=AX.X), [ssbb[k]], [stat4b[q4]])
                    kb.op(kb.pool, lambda: nc.gpsimd.tensor_scalar(out=stat4[q4][:, 1:2], in0=stat4[q4][:, 0:1], scalar1=-1.0, scalar2=None, op0=ALU.mult),
                          [stat4b[q4]], [stat4b[q4]])
                    kb.op(kb.pool, lambda: nc.gpsimd.memset(stat4[q4][:, 2:3], 0.0), [], [stat4b[q4]])

                def st_exp(n):
                    k, q4, p3 = n % 2, n % 4, n % 3
                    kb.op(kb.act, lambda: nc.scalar.activation(out=pbf3[p3][:], in_=ssb[k][:], func=AF.Exp, bias=stat4[q4][:, 1:2],
                                                               scale=1.0, accum_out=stat4[q4][:, 2:3]),
                          [ssbb[k], stat4b[q4]], [pbf3b[p3], stat4b[q4]])

                def st_norm(n):
                    q4, p3 = n % 4, n % 3
                    kb.op(kb.dve, lambda: v.reciprocal(out=stat4[q4][:, 3:4], in_=stat4[q4][:, 2:3]), [stat4b[q4]], [stat4b[q4]])
                    kb.op(kb.pool, lambda: nc.gpsimd.tensor_tensor(out=pbf3[p3][:], in0=pbf3[p3][:], in1=stat4[q4][:, 3:4].broadcast_to([128, 640]),
                                                                   op=ALU.mult), [pbf3b[p3], stat4b[q4]], [pbf3b[p3]])

                def st_tr(n):
                    k, p3 = n % 2, n % 3
                    kb._pre(kb.pe, [pbf3b[p3], cm.b_small], [cm.bbank[k]])
                    for jb in range(5):
                        ins = nc.tensor.transpose(cm.psb[:, k * 1024 + jb * 128:k * 1024 + (jb + 1) * 128],
                                                  pbf3[p3][:, jb * 128:(jb + 1) * 128], cm.ident[:])
                    tok = kb.pe.mark(ins)
                    kb._post(tok, [pbf3b[p3]], [cm.bbank[k]])
                    kb.op(kb.act, lambda: nc.scalar.copy(out=pT[k][:], in_=cm.psb[:, k * 1024:k * 1024 + 640]), [cm.bbank[k]], [pTb[k]])

                def st_pv(n):
                    h, i = its[n]
                    k, c, r0 = n % 2, h // 2, (h % 2) * 64
                    ob = cm.bank[4 + k]
                    kb._pre(kb.pe, [pTb[k], bV], [ob])
                    for jb in range(5):
                        ins = nc.tensor.matmul(cm.ps[:, (4 + k) * 512:(4 + k) * 512 + 128], V[:, i + jb, c * 128:(c + 1) * 128],
                                               pT[k][:, jb * 128:(jb + 1) * 128], start=(jb == 0), stop=(jb == 4))
                    tok = kb.pe.mark(ins)
                    kb._post(tok, [pTb[k], bV], [ob])
                    kb.op(kb.act, lambda: nc.scalar.copy(out=mixT[r0:r0 + 64, c, i * 128:(i + 1) * 128],
                                                         in_=cm.ps[r0:r0 + 64, (4 + k) * 512:(4 + k) * 512 + 128]), [ob], [bmix])

                stages = [(st_qk, 0), (st_front, 1), (st_exp, 2), (st_norm, 3), (st_tr, 4), (st_pv, 5)]
                for step in range(NIT + 5):
                    for fn, off in stages:
                        n = step - off
                        if 0 <= n < NIT:
                            fn(n)
                if DEBUG:
                    kb.dma(kb.sp, io["d_QT"], QT[:], reads=[bQ], is_output=True)
                    kb.dma(kb.sp, io["d_KT"], KT[:], reads=[bK], is_output=True)
                    kb.dma(kb.sp, io["d_V"], V[:], reads=[bV], is_output=True)
                    kb.dma(kb.sp, io["d_mix"], mixT, reads=[bmix], is_output=True)
        kb.barrier()
        with ExitStack() as es4:
            emit_outproj_ln_ffn(cm, es4, mixT, bmix, io["xT"], 0, io)
    kb.barrier()


def emit_gla_tail(kb, cm, io):
    nc = kb.nc
    with ExitStack() as es4:
        emit_outproj_ln_ffn(cm, es4, io["mixT_sb"], io["bmix"], io["xT"], 0, io, row_scale=io["gsm"][:, 4:12])
    kb.barrier()


def emit_gla(kb, cm, io, state_only, mixer_only=False):
    nc = kb.nc
    QS = 128.0 ** -0.5

    with ExitStack() as es0:
        v = nc.vector
        if "mixT_sb" in io:
            big = io["mixT_sb"]
        else:
            big = sb(nc, es0, "big", [128, NCH, T], BF16)
        mixT = big
        bmix = kb.buf("mixT")
        io["bmix"] = bmix
        with ExitStack() as es1:
            B = cm.b_small
            gsm = sb(nc, es1, "gsm", [128, 16], F32)
            cmask = sb(nc, es1, "cmask", [128, 512], F32)
            kb.dma(kb.sp, gsm[:], io["gsm"], writes=[B])
            kb.dma(kb.sp, cmask[:], io["cmask"], writes=[B])
            if "w_in_sb" in io:
                w_in, wgk2, bw = io["w_in_sb"], io["wgk2_sb"], io["bw"]
            else:
                w_in = sb(nc, es1, "w_in", [128, NCH, GLA_W], BF16)
                bw = kb.buf("w_in")
                wv_ = io["w_in"].rearrange("(c p) f -> p c f", p=128)
                for c in range(NCH):
                    kb.dma(kb.pool, w_in[:, c, :], wv_[:, c, :], writes=[bw])
                wgk2 = sb(nc, es1, "wgk2", [16, 512], BF16)
                kb.dma(kb.pool, wgk2[:], io["w_gk2"], writes=[bw])
            S = sb(nc, es1, "S", [128, 4, 256], F32)
            bS = [kb.buf(f"S{h}") for h in range(4)]
            Sa = sb(nc, es1, "Sa", [128, 4, 256], F32)
            bSa = [kb.buf(f"Sa{h}") for h in range(4)]
            nsum = sb(nc, es1, "nsum", [128, 8], F32)
            bns = kb.buf("nsum")
            kb.op(kb.dve, lambda: v.memset(nsum[:], 0.0), [], [bns])
            eps_r = sb(nc, es1, "eps_r", [128, 1], F32)
            kb.op(kb.dve, lambda: v.memset(eps_r[:], RMS_EPS), [], [B])
            if state_only:
                for h in range(4):
                    kb.op(kb.dve, lambda h=h: v.memset(S[:, h, :], 0.0), [], [bS[h]])
            else:
                M1 = sb(nc, es1, "M1", [128, 128], F32)
                M2 = sb(nc, es1, "M2", [128, 128], F32)
                ones256 = sb(nc, es1, "ones256", [128, 128], BF16)
                kb.dma(kb.sp, M1[:], io["M1"], writes=[B])
                kb.dma(kb.sp, M2[:], io["M2"], writes=[B])
                kb.dma(kb.pool, ones256[:], io["ones256"], writes=[B])
                pS = sb(nc, es1, "pS", [128, 4, 256], F32)
                pD = sb(nc, es1, "pD", [128, 8], F32)
                gm = sb(nc, es1, "gm", [128, 8], F32)
                bp = kb.buf("pS")
                kb.dma(kb.sp, gm[:, 0:4], io["gm"], writes=[bp])
                kb.dma(kb.sp, gm[:, 4:8], io["gm1"], writes=[bp])
                for h in range(4):
                    kb.op(kb.dve, lambda h=h: v.memset(S[:, h, :], 0.0), [], [bS[h]])
                sdg = io["sd_g"]
                for j in range(3):
                    kb.dma(kb.sp, pS[:].rearrange("p h v -> p (h v)"), sdg[j * 128:(j + 1) * 128, 0:1024], writes=[bp])
                    kb.dma(kb.sp, pD[:, 0:4], sdg[j * 128:(j + 1) * 128, 1024:1028], writes=[bp])
                    kb.op(kb.dve, lambda j=j: v.tensor_scalar(out=pD[:, 4:8], in0=pD[:, 0:4], scalar1=gm[:, j:j + 1], scalar2=gm[:, 4 + j:5 + j],
                                                             op0=ALU.mult, op1=ALU.add), [bp], [bp])
                    kb.op(kb.dve, lambda j=j: v.tensor_scalar(out=pS[:].rearrange("p h v -> p (h v)"), in0=pS[:].rearrange("p h v -> p (h v)"),
                                                             scalar1=gm[:, j:j + 1], scalar2=None, op0=ALU.mult), [bp], [bp])
                    for h in range(4):
                        kb.op(kb.dve, lambda h=h: v.scalar_tensor_tensor(out=S[:, h, :], in0=S[:, h, :], scalar=pD[:, 4 + h:5 + h],
                                                                       in1=pS[:, h, :], op0=ALU.mult, op1=ALU.add), [bp, bS[h]], [bS[h]])
                Sbf = sb(nc, es1, "Sbf", [128, 4, 4, 256], BF16)
                bSbf = [[[kb.buf(f"Sbf{h}_{p_}_{ab}") for ab in range(2)] for p_ in range(2)] for h in range(4)]
                for h in range(4):
                    kb.op(kb.act, lambda h=h: nc.scalar.copy(out=Sbf[:, h, 3, :], in_=S[:, h, :]), [bS[h]], [bSbf[h][1][1]])
            uTg = [sb(nc, es1, f"uTg{i}", [128, NCH, TG], BF16) for i in range(2)]
            bu = [kb.buf(f"uTg{i}") for i in range(2)]
            xs = [sb(nc, es1, f"xs{i}", [128, NCH, 256], F32) for i in range(2)]
            xsb = [kb.buf(f"xs{i}") for i in range(2)]
            gkT = sb(nc, es1, "gkT", [16, 512], BF16)
            bgk = kb.buf("gkT")
            NT = 2
            e1 = [sb(nc, es1, f"e1_{i}", [128, 512], F32) for i in range(NT)]
            ncum = [sb(nc, es1, f"ncum{i}", [128, 512], F32) for i in range(NT)]
            epos = [sb(nc, es1, f"epos{i}", [128, 512], F32) for i in range(NT)]
            eneg = [sb(nc, es1, f"eneg{i}", [128, 512], F32) for i in range(NT)]
            kn32 = [sb(nc, es1, f"kn32_{i}", [128, 512], F32) for i in range(NT)]
            be = [kb.buf(f"etmp{i}") for i in range(NT)]
            dec = sb(nc, es1, "dec", [128, 4, 8], F32)
            bdec = kb.buf("dec")
            kteT = sb(nc, es1, "kteT", [128, 4, TG], BF16)
            bkte = kb.buf("kteT")
            Vg = sb(nc, es1, "Vg", [128, 4, D], BF16)
            bVg = kb.buf("Vg")
            kte = [sb(nc, es1, f"kte{i}", [128, 128], BF16) for i in range(2)]
            bktet = [kb.buf(f"kte{i}") for i in range(2)]
            r1 = sb(nc, es1, "r1", [128, 1], F32)
            br1 = kb.buf("r1")
            if not state_only:
                qf = sb(nc, es1, "qf", [128, 4, TG], BF16)
                qn = sb(nc, es1, "qn", [128, 4, TG], BF16)
                kng = sb(nc, es1, "kng", [128, 4, TG], BF16)
                kp = sb(nc, es1, "kp", [128, 4, TG], BF16)
                bqk = kb.buf("qk")
                sgT = sb(nc, es1, "sgT", [128, NCH, TG], BF16)
                bsg = kb.buf("sgT")
                AT = [sb(nc, es1, f"AT{i}", [128, 128], BF16) for i in range(4)]
                bAT = [kb.buf(f"AT{i}") for i in range(4)]
                a1 = [sb(nc, es1, f"a1_{i}", [128, 128], F32) for i in range(2)]
                a2 = [sb(nc, es1, f"a2_{i}", [128, 128], F32) for i in range(2)]
                ba = [kb.buf(f"a12_{i}") for i in range(2)]
                osq = [sb(nc, es1, f"osq{i}", [128, 2, 128], BF16) for i in range(2)]
                bosq = [kb.buf(f"osq{i}") for i in range(2)]
                rstd = [sb(nc, es1, f"rstd{i}", [128, 128], F32) for i in range(2)]
                brs = [kb.buf(f"rstd{i}") for i in range(2)]
                t1 = [sb(nc, es1, f"t1_{i}", [128, 128], F32) for i in range(2)]
                bt1 = [kb.buf(f"t1_{i}") for i in range(2)]

            xsrc = io["xT"].rearrange("(c p) t -> p c t", p=128)
            nb = 0
            bsc = [cm.bank[2], cm.bank[2]]
            if state_only:
                bds = [[cm.bank[3], cm.bank[4]], [cm.bank[2], cm.bank[5]]]
                ds_col = [[1536, 2048], [1024, 2560]]
            else:
                bds = [[cm.bank[3], cm.bank[4]], [cm.bank[3], cm.bank[4]]]
                ds_col = [[1536, 2048], [1536 + 256, 2048 + 256]]
            bo = [cm.bank[0], cm.bank[1]]
            bms = [cm.bank[5], cm.bank[5]]
            if not state_only:
                o_sb = [sb(nc, es1, f"o_sb{i}", [128, 256], F32) for i in range(3)]
                bosb = [kb.buf(f"o_sb{i}") for i in range(3)]

            def proj(lhs_cols, ug, bug, M=128):
                nonlocal nb
                bk = nb % 2
                nb += 1
                kb._pre(kb.pe, [bw, bug], [cm.bank[bk]])
                for c in range(NCH):
                    ins = nc.tensor.matmul(cm.ps[0:M, bk * 512:(bk + 1) * 512], w_in[:, c, lhs_cols], ug[:, c, :],
                                           start=(c == 0), stop=(c == NCH - 1))
                tok = kb.pe.mark(ins)
                kb._post(tok, [bw, bug], [cm.bank[bk]])
                return bk

            it = 0
            et = 0
            for g in range(NG):
                ug, bug = uTg[g % 2], bu[g % 2]
                for hf in range(2):
                    s_ = (2 * g + hf) % 2
                    tsl = slice(g * TG + hf * 256, g * TG + (hf + 1) * 256)
                    kb.dma(kb.sp, xs[s_][:], xsrc[:, :, tsl], writes=[xsb[s_]])
                    for c in range(NCH):
                        if c % 2 == 0:
                            kb.op(kb.act, lambda c=c, s_=s_, hf=hf: nc.scalar.activation(
                                out=ug[:, c, hf * 256:(hf + 1) * 256], in_=xs[s_][:, c, :], func=AF.Identity,
                                scale=cm.der[:, c:c + 1], bias=cm.der[:, 8 + c:9 + c]), [xsb[s_], B], [bug])
                        else:
                            kb.op(kb.dve, lambda c=c, s_=s_, hf=hf: v.tensor_scalar(
                                out=ug[:, c, hf * 256:(hf + 1) * 256], in0=xs[s_][:, c, :],
                                scalar1=cm.der[:, c:c + 1], scalar2=cm.der[:, 8 + c:9 + c], op0=ALU.mult, op1=ALU.add),
                                [xsb[s_], B], [bug])
                bk = proj(slice(3072, 3088), ug, bug, M=16)
                kb.op(kb.act, lambda bk=bk: nc.scalar.copy(out=gkT[:], in_=cm.ps[0:16, bk * 512:(bk + 1) * 512]), [cm.bank[bk]], [bgk])
                for h in range(4):
                    e = et % NT
                    et += 1
                    bk = nb % 2
                    nb += 1
                    kb._pre(kb.pe, [bw, bgk], [cm.bank[bk]])
                    ins = nc.tensor.matmul(cm.bank_ap(bk), wgk2[:, h * 128:(h + 1) * 128], gkT[:], start=True, stop=True)
                    tok = kb.pe.mark(ins)
                    kb._post(tok, [bw, bgk], [cm.bank[bk]])
                    kb.op(kb.act, lambda bk=bk, e=e, h=h: nc.scalar.activation(out=e1[e][:], in_=cm.bank_ap(bk), func=AF.Exp,
                                                                                  scale=-1.0, bias=gsm[:, h:h + 1]), [cm.bank[bk], B], [be[e]])
                    kb.op(kb.act, lambda e=e: nc.scalar.activation(out=e1[e][:], in_=e1[e][:], func=AF.Ln, scale=1.0, bias=1.0), [be[e]], [be[e]])
                    kb.op(kb.dve, lambda e=e: v.tensor_tensor_scan(out=ncum[e][:], data0=cmask[:], data1=e1[e][:], initial=0.0,
                                                                   op0=ALU.mult, op1=ALU.add), [be[e], B], [be[e]])
                    if state_only:
                        kb.op(kb.act, lambda e=e, h=h: nc.scalar.activation(out=dec[:, h, :], in_=ncum[e][:, 63::64], func=AF.Exp, scale=-1.0 / 16.0),
                              [be[e]], [bdec])
                        kb.op(kb.act, lambda e=e: nc.scalar.activation(out=eneg[e][:], in_=ncum[e][:], func=AF.Exp, scale=1.0 / 16.0), [be[e]], [be[e]])
                    else:
                        kb.op(kb.act, lambda e=e: nc.scalar.activation(out=epos[e][:], in_=ncum[e][:], func=AF.Exp, scale=-1.0 / 16.0), [be[e]], [be[e]])
                        kb.op(kb.act, lambda e=e: nc.scalar.activation(out=eneg[e][:], in_=ncum[e][:], func=AF.Exp, scale=1.0 / 16.0), [be[e]], [be[e]])
                        kb.op(kb.dve, lambda e=e, h=h: v.tensor_copy(out=dec[:, h, :], in_=epos[e][:, 63::64]), [be[e]], [bdec])
                    if state_only:
                        kb.op(kb.dve, lambda e=e: v.reduce_sum(out=r1[:], in_=ncum[e][:, 63::64], axis=AX.X), [be[e]], [br1])
                        kb.op(kb.dve, lambda h=h: v.tensor_tensor(out=nsum[:, h:h + 1], in0=nsum[:, h:h + 1], in1=r1[:], op=ALU.add), [br1, bns], [bns])
                    bk = proj(slice(512 + h * 128, 512 + (h + 1) * 128), ug, bug)
                    kb.op(kb.dve, lambda bk=bk, e=e: v.tensor_tensor(out=kn32[e][:], in0=cm.bank_ap(bk), in1=eneg[e][:], op=ALU.mult),
                          [cm.bank[bk], be[e]], [be[e]])
                    for n in range(8):
                        kb.op(kb.dve, lambda n=n, e=e, h=h: v.tensor_scalar(out=kteT[:, h, n * 64:(n + 1) * 64], in0=kn32[e][:, n * 64:(n + 1) * 64],
                                                                         scalar1=dec[:, h, n:n + 1], scalar2=None, op0=ALU.mult),
                              [be[e], bdec], [bkte])
                    if not state_only:
                        kb.op(kb.act, lambda e=e, h=h: nc.scalar.copy(out=kng[:, h, :], in_=kn32[e][:]), [be[e]], [bqk])
                        kb.op(kb.dve, lambda bk=bk, e=e, h=h: v.tensor_tensor(out=kp[:, h, :], in0=cm.bank_ap(bk), in1=epos[e][:], op=ALU.mult),
                              [cm.bank[bk], be[e]], [bqk])
                        bk = proj(slice(h * 128, (h + 1) * 128), ug, bug)
                        kb.op(kb.dve, lambda bk=bk, e=e, h=h: v.scalar_tensor_tensor(out=qf[:, h, :], in0=cm.bank_ap(bk), scalar=QS, in1=epos[e][:],
                                                                                      op0=ALU.mult, op1=ALU.mult), [cm.bank[bk], be[e]], [bqk])
                        kb.op(kb.dve, lambda bk=bk, e=e, h=h: v.scalar_tensor_tensor(out=qn[:, h, :], in0=cm.bank_ap(bk), scalar=QS, in1=eneg[e][:],
                                                                                      op0=ALU.mult, op1=ALU.mult), [cm.bank[bk], be[e]], [bqk])
                for tt in range(4):
                    for half in range(2):
                        bk = nb % 2
                        nb += 1
                        kb._pre(kb.pe, [bw, bug], [cm.bank[bk]])
                        for c in range(NCH):
                            ins = nc.tensor.matmul(cm.bank_ap(bk), ug[:, c, tt * 128:(tt + 1) * 128],
                                                   w_in[:, c, 1024 + half * 512:1024 + (half + 1) * 512], start=(c == 0), stop=(c == NCH - 1))
                        tok = kb.pe.mark(ins)
                        kb._post(tok, [bw, bug], [cm.bank[bk]])
                        kb.op(kb.act, lambda bk=bk, tt=tt, half=half: nc.scalar.copy(out=Vg[:, tt, half * 512:(half + 1) * 512], in_=cm.bank_ap(bk)),
                              [cm.bank[bk]], [bVg])
                if not state_only:
                    for vc in range(8):
                        bk = proj(slice(2048 + vc * 128, 2048 + (vc + 1) * 128), ug, bug)
                        kb.op(kb.act, lambda bk=bk, vc=vc: nc.scalar.activation(out=sgT[:, vc, :], in_=cm.bank_ap(bk), func=AF.Silu),
                              [cm.bank[bk]], [bsg])
                def p1(n):
                    tt, h = n // 4, n % 4
                    tl = slice(tt * 128, (tt + 1) * 128)
                    k = n % 2
                    kb._pre(kb.pe, [bkte, B], [cm.bbank[k]])
                    ins = nc.tensor.transpose(cm.psb[:, k * 1024:k * 1024 + 128], kteT[:, h, tl], cm.ident[:])
                    tok = kb.pe.mark(ins)
                    kb._post(tok, [bkte], [cm.bbank[k]])
                    if not state_only:
                        sc = bsc[k]
                        kb._pre(kb.pe, [bqk], [sc])
                        nc.tensor.matmul(cm.ps[:, 1024 + k * 256:1024 + k * 256 + 128], kng[:, h, tl], qf[:, h, tl], start=True, stop=True)
                        ins = nc.tensor.matmul(cm.ps[:, 1024 + k * 256 + 128:1024 + (k + 1) * 256], kp[:, h, tl], qn[:, h, tl], start=True, stop=True)
                        tok = kb.pe.mark(ins)
                        kb._post(tok, [bqk], [sc])

                def p2(n):
                    k = n % 2
                    kb.op(kb.act, lambda: nc.scalar.copy(out=kte[k][:], in_=cm.psb[:, k * 1024:k * 1024 + 128]), [cm.bbank[k]], [bktet[k]])
                    if not state_only:
                        sc = bsc[k]
                        q = n % 4
                        kb.op(kb.dve, lambda: v.tensor_tensor(out=a1[k][:], in0=cm.ps[:, 1024 + k * 256:1024 + k * 256 + 128], in1=M1[:], op=ALU.mult),
                              [sc, B], [ba[k]])
                        kb.op(kb.dve, lambda: v.tensor_tensor(out=a2[k][:], in0=cm.ps[:, 1024 + k * 256 + 128:1024 + (k + 1) * 256], in1=M2[:], op=ALU.mult),
                              [sc, B], [ba[k]])
                        kb.op(kb.dve, lambda: v.tensor_tensor(out=AT[q][:], in0=a1[k][:], in1=a2[k][:], op=ALU.add), [ba[k]], [bAT[q]])

                def p3(n):
                    tt, h = n // 4, n % 4
                    k = n % 2
                    for ch in range(2):
                        rows = slice(ch * 64, (ch + 1) * 64)
                        bd = bds[k][ch]
                        c0 = ds_col[k][ch]
                        kb._pre(kb.pe, [bktet[k], bVg], [bd])
                        ins = nc.tensor.matmul(cm.ps[:, c0:c0 + 256], kte[k][rows, :], Vg[rows, tt, h * 256:(h + 1) * 256], start=True, stop=True)
                        tok = kb.pe.mark(ins)
                        kb._post(tok, [bktet[k], bVg], [bd])

                def p4(n):
                    tt, h = n // 4, n % 4
                    k = n % 2
                    par = (g * 4 + tt) % 2
                    nn = tt * 2
                    ca0 = ds_col[k][0]
                    cb0 = ds_col[k][1]
                    kb.op(kb.dve, lambda: v.scalar_tensor_tensor(out=Sa[:, h, :], in0=S[:, h, :], scalar=dec[:, h, nn:nn + 1],
                                                                  in1=cm.ps[:, ca0:ca0 + 256], op0=ALU.mult, op1=ALU.add),
                          [bds[k][0], bS[h], bdec], [bSa[h]])
                    kb.op(kb.dve, lambda: v.scalar_tensor_tensor(out=S[:, h, :], in0=Sa[:, h, :], scalar=dec[:, h, nn + 1:nn + 2],
                                                                  in1=cm.ps[:, cb0:cb0 + 256], op0=ALU.mult, op1=ALU.add),
                          [bds[k][1], bSa[h], bdec], [bS[h]])
                    if not state_only:
                        kb.op(kb.act, lambda: nc.scalar.copy(out=Sbf[:, h, par * 2, :], in_=Sa[:, h, :]), [bSa[h]], [bSbf[h][par][0]])
                        kb.op(kb.pool, lambda: nc.gpsimd.tensor_copy(out=Sbf[:, h, par * 2 + 1, :], in_=S[:, h, :]), [bS[h]], [bSbf[h][par][1]])

                def p5(n):
                    tt, h = n // 4, n % 4
                    q = n % 4
                    par = (g * 4 + tt) % 2
                    bo_ = bo[n % 2]
                    o0 = (n % 2) * 512
                    prev_b, cur_a = bSbf[h][1 - par][1], bSbf[h][par][0]
                    kb._pre(kb.pe, [bAT[q], bVg, prev_b, cur_a, bqk], [bo_])
                    for vc in range(2):
                        nc.tensor.matmul(cm.ps[:, o0 + vc * 128:o0 + (vc + 1) * 128], Vg[:, tt, h * 256 + vc * 128:h * 256 + (vc + 1) * 128],
                                         AT[q][:], start=(vc == 0), stop=False, skip_group_check=True)
                    for ch in range(2):
                        sidx = (1 - par) * 2 + 1 if ch == 0 else par * 2
                        for vc in range(2):
                            ins = nc.tensor.matmul(cm.ps[:, o0 + vc * 128 + ch * 64:o0 + vc * 128 + (ch + 1) * 64],
                                                   Sbf[:, h, sidx, vc * 128:(vc + 1) * 128], qf[:, h, tt * 128 + ch * 64:tt * 128 + (ch + 1) * 64],
                                                   start=False, stop=(ch == 1), skip_group_check=True)
                    tok = kb.pe.mark(ins)
                    kb._post(tok, [bAT[q], bVg, prev_b, cur_a, bqk], [bo_])

                def p6(n):
                    k = n % 2
                    o0 = k * 512
                    kb.op(kb.act, lambda: nc.scalar.activation(out=osq[k][:], in_=cm.ps[:, o0:o0 + 256].rearrange("p (a b) -> p a b", a=2),
                                                               func=AF.Square), [bo[k]], [bosq[k]])
                    kb.op(kb.act, lambda: nc.scalar.copy(out=o_sb[n % 3][:], in_=cm.ps[:, o0:o0 + 256]), [bo[k]], [bosb[n % 3]])

                def p7(n):
                    k = n % 2
                    bm_ = bms[k]
                    m0 = 2560 + k * 128
                    kb._pre(kb.pe, [bosq[k], B], [bm_])
                    nc.tensor.matmul(cm.ps[:, m0:m0 + 128], ones256[:], osq[k][:, 0, :], start=True, stop=False)
                    ins = nc.tensor.matmul(cm.ps[:, m0:m0 + 128], ones256[:], osq[k][:, 1, :], start=False, stop=True)
                    tok = kb.pe.mark(ins)
                    kb._post(tok, [bosq[k]], [bm_])

                def p8(n):
                    tt, h = n // 4, n % 4
                    k = n % 2
                    tl = slice(tt * 128, (tt + 1) * 128)
                    m0 = 2560 + k * 128
                    kb.op(kb.act, lambda: nc.scalar.activation(out=rstd[k][:], in_=cm.ps[:, m0:m0 + 128], func=AF.Ln, bias=eps_r[:, 0:1], scale=1.0),
                          [bms[k], B], [brs[k]])
                    kb.op(kb.act, lambda: nc.scalar.activation(out=rstd[k][:], in_=rstd[k][:], func=AF.Exp, scale=-0.5), [brs[k]], [brs[k]])
                    for vc in range(2):
                        col = h * 2 + vc
                        kb.op(kb.dve, lambda vc=vc: v.tensor_tensor(out=t1[vc][:], in0=o_sb[n % 3][:, vc * 128:(vc + 1) * 128], in1=rstd[k][:],
                                                                    op=ALU.mult), [bosb[n % 3], brs[k]], [bt1[vc]])
                        kb.op(kb.pool, lambda vc=vc, col=col: nc.gpsimd.tensor_tensor(
                            out=mixT[:, col, g * TG + tl.start:g * TG + tl.stop], in0=t1[vc][:], in1=sgT[:, col, tl], op=ALU.mult),
                            [bt1[vc], bsg], [bmix])

                stages = [p1, p2, p3, p4] if state_only else [p1, p2, p3, p4, p5, p6, p7, p8]
                for step in range(16 + len(stages) - 1):
                    for off, fn in enumerate(stages):
                        n = step - off
                        if 0 <= n < 16:
                            fn(n)
            if state_only:
                for h in range(4):
                    kb.dma(kb.sp, io["sd_in"][:, h * 256:(h + 1) * 256], S[:, h, :], reads=[bS[h]], is_output=True)
                kb.op(kb.act, lambda: nc.scalar.activation(out=nsum[:, 4:8], in_=nsum[:, 0:4], func=AF.Exp, scale=-1.0 / 16.0), [bns], [bns])
                kb.dma(kb.sp, io["sd_in"][:, 1024:1028], nsum[:, 4:8], reads=[bns], is_output=True)
        if not state_only and not mixer_only:
            kb.barrier()
            with ExitStack() as es4:
                emit_outproj_ln_ffn(cm, es4, mixT, bmix, io["xT"], 0, io, row_scale=io["gsm"][:, 4:12])
    kb.barrier()


GROUPS = [[0, 1, 2, 3], [4, 5, 6, 7]]


def emit_mods(kb, cm, io, es):
    nc = kb.nc
    v = nc.vector
    c_sb = sb(nc, es, "c_sb", [128, NCH], F32)
    ca = sb(nc, es, "ca", [128, NCH], BF16)
    bA = sb(nc, es, "bA", [128, 48], F32)
    res = sb(nc, es, "mres", [128, 48], F32)
    wl = [sb(nc, es, f"wl{i}", [128, NCH, 1536], BF16) for i in range(2)]
    bc, bwl, br = kb.buf("c"), [kb.buf(f"wl{i}") for i in range(2)], kb.buf("mres")
    kb.dma(kb.sp, c_sb[:], io["c_pc"], writes=[bc])
    kb.dma(kb.sp, bA[:], io["bA_pc"], writes=[bc])
    kb.op(kb.act, lambda: nc.scalar.activation(out=ca[:], in_=c_sb[:], func=AF.Silu), [bc], [bc])
    wv = io["wA"].rearrange("(c p) f -> p c f", p=128)
    for l in range(DEPTH):
        s_ = l % 2
        kb.dma(kb.pool, wl[s_][:], wv[:, :, l * 1536:(l + 1) * 1536], writes=[bwl[s_]])
        bk = cm.bank[l % 2]
        kb._pre(kb.pe, [bc, bwl[s_]], [bk])
        for jj in range(12):
            for c in range(NCH):
                ins = nc.tensor.matmul(cm.ps[:, (l % 2) * 512 + jj:(l % 2) * 512 + jj + 1], wl[s_][:, c, jj * 128:(jj + 1) * 128], ca[:, c:c + 1],
                                       start=(c == 0), stop=(c == NCH - 1))
        tok = kb.pe.mark(ins)
        kb._post(tok, [bc, bwl[s_]], [bk])
        kb.op(kb.dve, lambda l=l: v.tensor_tensor(out=res[:, l * 12:(l + 1) * 12], in0=cm.ps[:, (l % 2) * 512:(l % 2) * 512 + 12],
                                                  in1=bA[:, l * 12:(l + 1) * 12], op=ALU.add), [bk, bc], [br])
    kb.dma(kb.sp, io["mods_in"], res[:], reads=[br])


def build_fused():
    kb = KB()
    nc = kb.nc
    io = {}

    def inp(name, shape, dt=F32):
        io[name] = nc.dram_tensor(name, shape, dt, kind="ExternalInput").ap()

    def scratch(name, shape, dt=F32):
        io[name] = nc.dram_tensor(name, shape, dt, kind="Internal").ap()

    inp("xT", [D, T])
    inp("c_pc", [128, NCH])
    inp("wA", [D, 6144])
    inp("bA_pc", [128, 48])
    inp("sel", [128, 4])
    inp("hb", [128, HALO])
    inp("gm", [128, 4])
    inp("gm1", [128, 4])
    for nm in ("onesD", "ident", "M1", "M2", "ones256"):
        inp(nm, [128, 128])
    inp("cmask", [128, 512])
    for l in range(DEPTH):
        inp(f"lnp{l}", [128, 32])
        inp(f"ff_w1_{l}", [D, DFF])
        inp(f"ff_w2_{l}", [DFF, D])
    for j in range(2):
        inp(f"gw_in{j}", [D, GLA_W])
        inp(f"gw_gk2{j}", [16, 512])
        inp(f"gsm{j}", [128, 16])
        inp(f"gw_out{j}", [D, D])
        inp(f"aw_in{j}", [D, 3 * D])
        inp(f"ab_qk{j}", [128, 16])
        inp(f"abv_bc{j}", [128, D])
        inp(f"atiles{j}", [16, 128, 640])
        inp(f"ahalo{j}", [4, 16, 128, 640])
        inp(f"aw_out{j}", [D, D])
    io["xT_final"] = nc.dram_tensor("xT_final", [D, T], F32, kind="ExternalOutput").ap()
    scratch("mods_in", [128, 48])
    scratch("mods_g", [4 * 128, 48])
    scratch("sd_in", [128, 1028])
    scratch("sd_g", [4 * 128, 1028])
    for hh in range(2):
        scratch(f"halo_in{hh}", [D, 256])
        scratch(f"hgat{hh}", [4 * D, 256])
    scratch("xbuf0", [D, T])
    scratch("xbuf1", [D, T])

    with ExitStack() as es0:
        cm = Common(kb, es0, io)
        pre_esm = ExitStack()
        pre_mixT = sb(nc, pre_esm, "bigg", [128, NCH, T], BF16)
        pre_esw = ExitStack()
        pre_w_in = sb(nc, pre_esw, "w_in", [128, NCH, GLA_W], BF16)
        pre_wgk2 = sb(nc, pre_esw, "wgk2", [16, 512], BF16)
        pre_bw = kb.buf("w_in")
        with ExitStack() as esm:
            emit_mods(kb, cm, io, esm)
        wv0 = io["gw_in0"].rearrange("(c p) f -> p c f", p=128)
        for c in range(NCH):
            kb.dma(kb.pool, pre_w_in[:, c, :], wv0[:, c, :], writes=[pre_bw])
        kb.dma(kb.pool, pre_wgk2[:], io["gw_gk20"], writes=[pre_bw])
        kb.barrier()
        kb.collective(io["mods_in"], io["mods_g"], GROUPS)
        kb.barrier()
        x_cur = io["xT"]
        for l in range(DEPTH):
            j = l // 2
            msrc = io["mods_g"].rearrange("(r p) f -> p r f", p=128)[:, :, l * 12:(l + 1) * 12]
            cm.load_layer(msrc, io[f"lnp{l}"], mods_view=lambda m: m[:].rearrange("p (r f) -> p r f", r=4))
            x_out = io["xT_final"] if l == DEPTH - 1 else io[f"xbuf{l % 2}"]
            lio = {"xT": x_cur, "xT_out": x_out, "ff_w1": io[f"ff_w1_{l}"], "ff_w2": io[f"ff_w2_{l}"]}
            if l % 2 == 0:
                lio.update({"w_in": io[f"gw_in{j}"], "w_gk2": io[f"gw_gk2{j}"], "gsm": io[f"gsm{j}"], "cmask": io["cmask"],
                            "M1": io["M1"], "M2": io["M2"], "ones256": io["ones256"], "w_out": io[f"gw_out{j}"],
                            "sd_in": io["sd_in"], "sd_g": io["sd_g"], "gm": io["gm"], "gm1": io["gm1"]})
                if l == 0:
                    esm_, esw = pre_esm, pre_esw
                    lio["mixT_sb"] = pre_mixT
                    w_in_sb, wgk2_sb, bw_ = pre_w_in, pre_wgk2, pre_bw
                else:
                    esm_ = ExitStack()
                    lio["mixT_sb"] = sb(nc, esm_, "bigg", [128, NCH, T], BF16)
                    esw = ExitStack()
                    w_in_sb = sb(nc, esw, "w_in", [128, NCH, GLA_W], BF16)
                    wgk2_sb = sb(nc, esw, "wgk2", [16, 512], BF16)
                    bw_ = kb.buf("w_in")
                    wv_ = lio["w_in"].rearrange("(c p) f -> p c f", p=128)
                    for c in range(NCH):
                        kb.dma(kb.pool, w_in_sb[:, c, :], wv_[:, c, :], writes=[bw_])
                    kb.dma(kb.pool, wgk2_sb[:], lio["w_gk2"], writes=[bw_])
                with esw:
                    lio.update({"w_in_sb": w_in_sb, "wgk2_sb": wgk2_sb, "bw": bw_})
                    emit_gla(kb, cm, lio, True)
                    kb.collective(io["sd_in"], io["sd_g"], GROUPS)
                    kb.barrier()
                    emit_gla(kb, cm, lio, False, mixer_only=True)
                kb.barrier()
                emit_gla_tail(kb, cm, lio)
                esm_.close()
                for hh in range(2):
                    kb.dma(kb.sp, io[f"halo_in{hh}"], x_out[:, T - HALO + hh * 256:T - HALO + (hh + 1) * 256])
                kb.barrier()
                halo_toks = [kb.collective(io[f"halo_in{hh}"], io[f"hgat{hh}"], GROUPS, wait=False) for hh in range(2)]
            else:
                lio.update({"w_in": io[f"aw_in{j}"], "b_qk": io[f"ab_qk{j}"], "bv_bc": io[f"abv_bc{j}"], "bias_tiles": io[f"atiles{j}"],
                            "w_out": io[f"aw_out{j}"], "hgat0": io["hgat0"], "hgat1": io["hgat1"], "sel": io["sel"], "hb": io["hb"],
                            "bias_halo": io[f"ahalo{j}"],
                            "halo_toks": halo_toks})
                emit_att(kb, cm, lio)
            x_cur = x_out
        kb.finish()
    return nc


_PROGS = {}


def _prog(name, fn):
    if name not in _PROGS:
        _PROGS[name] = fn()
    return _PROGS[name]


def _run(nc, in_maps):
    res = run_bass_kernel_spmd(nc, in_maps, core_ids=list(range(8)))
    return res.results


def _pc(vec):
    return np.ascontiguousarray(np.asarray(vec, np.float32).reshape(-1, 128).T)


def consts():
    onesD = np.full((128, 128), 1.0 / D, np.float32)
    ident = np.eye(128, dtype=np.float32)
    return onesD, ident


def att_bias_tiles(rel_bias):
    t = np.arange(128)[:, None]
    j = np.arange(640)[None, :]
    dist = 512 + t - j
    idx = np.clip(dist, -128, 128) + 128
    qa = t // 64
    kc = j // 64
    valid = (kc >= qa) & (kc <= qa + 8)
    tiles = np.asarray(rel_bias, np.float32)[:, idx]
    tiles = np.where(valid[None], tiles, np.float32(NEG)).astype(np.float32)
    return np.ascontiguousarray(tiles)


def gla_consts():
    t = np.arange(128)
    same = (t[:, None] // 64) == (t[None, :] // 64)
    M1 = (same & (t[:, None] <= t[None, :])).astype(np.float32)
    M2 = (same & (t[:, None] > t[None, :])).astype(np.float32)
    cm = np.ones((128, 512), np.float32)
    cm[:, 0::64] = 0.0
    ones256 = np.full((128, 128), 1.0 / 256.0, np.float32)
    return M1, M2, cm, ones256


def kernel(x, c, w_ada, b_ada, ln_g, ln_b, gla_w_in, gla_w_gk2, gla_b_gk, gla_g_norm, gla_w_out,
           att_w_in, att_b_in, att_rel_bias, att_w_out, ff_w1, ff_w2):
    f = lambda a: np.ascontiguousarray(np.asarray(a, np.float32))
    x, c, w_ada, b_ada, ln_g, ln_b = f(x), f(c), f(w_ada), f(b_ada), f(ln_g), f(ln_b)
    nc = _prog("fused", build_fused)
    onesD, ident = consts()
    M1, M2, cmask, ones256 = gla_consts()
    shared = {"onesD": onesD, "ident": ident, "M1": M1, "M2": M2, "ones256": ones256, "cmask": cmask}
    for l in range(DEPTH):
        shared[f"lnp{l}"] = np.ascontiguousarray(np.concatenate([_pc(ln_g[l, 0]), _pc(ln_b[l, 0]), _pc(ln_g[l, 1]), _pc(ln_b[l, 1])], axis=1))
        shared[f"ff_w1_{l}"] = f(ff_w1[l])
        shared[f"ff_w2_{l}"] = f(ff_w2[l])
    for j in range(2):
        gsm = np.zeros((128, 16), np.float32)
        gsm[:, 0:4] = -_pc(gla_b_gk[j])
        gsm[:, 4:12] = _pc(np.asarray(gla_g_norm[j], np.float32).reshape(-1))
        b_in = np.asarray(att_b_in[j], np.float32)
        shared[f"gw_in{j}"] = f(gla_w_in[j])
        shared[f"gw_gk2{j}"] = f(gla_w_gk2[j])
        shared[f"gsm{j}"] = gsm
        shared[f"gw_out{j}"] = f(gla_w_out[j])
        shared[f"aw_in{j}"] = f(att_w_in[j])
        shared[f"ab_qk{j}"] = np.ascontiguousarray(np.concatenate([_pc(b_in[0:D]), _pc(b_in[D:2 * D])], axis=1))
        shared[f"abv_bc{j}"] = np.ascontiguousarray(np.broadcast_to(b_in[2 * D:3 * D][None, :], (128, D)))
        shared[f"atiles{j}"] = att_bias_tiles(att_rel_bias[j])
        shared[f"aw_out{j}"] = f(att_w_out[j])
    in_maps = []
    for cid in range(8):
        b, r = cid // 4, cid % 4
        m = dict(shared)
        m["xT"] = np.ascontiguousarray(x[b, r * T:(r + 1) * T, :].T)
        m["c_pc"] = _pc(c[b])
        m["wA"] = np.ascontiguousarray(np.concatenate([w_ada[l][:, r * 1536:(r + 1) * 1536] for l in range(DEPTH)], axis=1))
        m["bA_pc"] = np.ascontiguousarray(np.concatenate([_pc(b_ada[l][r * 1536:(r + 1) * 1536]) for l in range(DEPTH)], axis=1))
        sel = np.zeros((128, 4), np.float32)
        if r > 0:
            sel[:, r - 1] = 1.0
        m["sel"] = sel
        m["hb"] = np.full((128, HALO), NEG if r == 0 else 0.0, np.float32)
        for j in range(2):
            tl_ = shared[f"atiles{j}"]
            hv = np.empty((4,) + tl_.shape, np.float32)
            for i in range(4):
                hv[i] = tl_
                if r == 0:
                    hv[i][:, :, 0:HALO - 128 * i] = NEG
            m[f"ahalo{j}"] = hv
        gm = np.zeros((128, 4), np.float32)
        gm[:, 0:r] = 1.0
        m["gm"] = gm
        m["gm1"] = np.ascontiguousarray(1.0 - gm)
        in_maps.append(m)
    res = _run(nc, in_maps)
    out = np.stack([np.concatenate([res[b * 4 + r]["xT_final"].T for r in range(4)], axis=0) for b in range(2)])
    return np.ascontiguousarray(out.astype(np.float32))
```
